# Optimizing a Trainium2 kernel written in Bass

```python
import math
import numpy as np
import jax
import jax.numpy as jnp
from jax import lax

D_MODEL = 1024
BATCH = 16
SEQ = 2048
DEPTH = 4

GRID_W = 64
CTX_LEN = 256
DN_HEADS = 4
DN_HEAD_DIM = 128
DN_WIDTH = DN_HEADS * DN_HEAD_DIM
DN_CONV = 3
DN_CHUNK = 64
SC_WIDTH = 512
SC_CONV = 3
NA_HEADS = 8
NA_HEAD_DIM = 64
NA_WIDTH = NA_HEADS * NA_HEAD_DIM
NA_WIN_R = 8
NA_WIN_C = 16
NA_QBLK_C = 16
NA_KBLK_C = 32
D_FF = 2816
FFN_CONV = 3
N_BRANCH = 3
ROPE_BASE = 10000.0
EPS = 1e-6
IN_SPLITS = (3 * DN_WIDTH, DN_WIDTH, 2 * DN_HEADS, 2 * DN_HEADS, 3 * SC_WIDTH, 3 * NA_WIDTH, N_BRANCH * D_MODEL)
N_IN = sum(IN_SPLITS)

kernel_name = 'hybrid_dit_gdn_shortconv_natten_convffn'


def rms_norm(x, g):
    x32 = x.astype(jnp.float32)
    y = x32 * lax.rsqrt(jnp.mean(x32 * x32, axis=-1, keepdims=True) + EPS)
    return (y * g.astype(jnp.float32)).astype(x.dtype)


def l2norm(x):
    x32 = x.astype(jnp.float32)
    return x32 * lax.rsqrt(jnp.sum(x32 * x32, axis=-1, keepdims=True) + EPS)


def dwconv(x, w):
    K = w.shape[0]
    T = x.shape[1]
    lo = (K - 1) // 2
    xp = jnp.pad(x, ((0, 0), (lo, K - 1 - lo), (0, 0)))
    out = xp[:, 0:T] * w[0]
    for i in range(1, K):
        out = out + xp[:, i:i + T] * w[i]
    return out


def split_cols(p):
    pts = np.cumsum(IN_SPLITS)[:-1].tolist()
    return jnp.split(p, pts, axis=-1)


def axial_rope(T, dim):
    t = jnp.arange(T, dtype=jnp.int32)
    rows = (t // GRID_W).astype(jnp.float32)
    cols = (t % GRID_W).astype(jnp.float32)
    nf = dim // 4
    inv = ROPE_BASE ** (-jnp.arange(nf, dtype=jnp.float32) / nf)
    ang = jnp.concatenate([rows[:, None] * inv, cols[:, None] * inv], axis=-1)
    return jnp.cos(ang), jnp.sin(ang)


def apply_rope(x, cos, sin):
    x1 = x[..., 0::2]
    x2 = x[..., 1::2]
    return jnp.stack([x1 * cos - x2 * sin, x1 * sin + x2 * cos], axis=-1).reshape(x.shape)


def chunk_gated_delta(q, k, v, g, beta, S0):
    B, H, T, dk = q.shape
    dv = v.shape[-1]
    C = DN_CHUNK
    N = T // C
    q = (q * dk ** -0.5).reshape(B, H, N, C, dk)
    k = k.reshape(B, H, N, C, dk)
    v = v.reshape(B, H, N, C, dv)
    beta = beta.reshape(B, H, N, C)
    gc = jnp.cumsum(g.reshape(B, H, N, C), axis=-1)
    tri_incl = jnp.tril(jnp.ones((C, C), dtype=bool))
    tri_strict = jnp.tril(jnp.ones((C, C), dtype=bool), -1)
    diff = gc[..., :, None] - gc[..., None, :]
    decay = jnp.where(tri_incl, jnp.exp(jnp.where(tri_incl, diff, 0.0)), 0.0)
    kk = jnp.einsum('bhncd,bhnsd->bhncs', k, k)
    lmat = jnp.where(tri_strict, beta[..., :, None] * kk * decay, 0.0) + jnp.eye(C, dtype=jnp.float32)
    u = lax.linalg.triangular_solve(lmat, v * beta[..., None], left_side=True, lower=True, unit_diagonal=True)
    w = lax.linalg.triangular_solve(lmat, k * (beta * jnp.exp(gc))[..., None], left_side=True, lower=True, unit_diagonal=True)
    qk = jnp.einsum('bhncd,bhnsd->bhncs', q, k) * decay
    q_dec = q * jnp.exp(gc)[..., None]
    k_dec = k * jnp.exp(gc[..., -1:] - gc)[..., None]
    g_last = jnp.exp(gc[..., -1])

    def step(S, inp):
        u_i, w_i, qk_i, qd_i, kd_i, gl_i = inp
        v_new = u_i - jnp.einsum('bhck,bhkv->bhcv', w_i, S)
        o_i = jnp.einsum('bhck,bhkv->bhcv', qd_i, S) + jnp.einsum('bhcs,bhsv->bhcv', qk_i, v_new)
        S = S * gl_i[..., None, None] + jnp.einsum('bhck,bhcv->bhkv', kd_i, v_new)
        return S, o_i

    to_scan = lambda t: jnp.moveaxis(t, 2, 0)
    S_fin, o = lax.scan(step, S0, (to_scan(u), to_scan(w), to_scan(qk), to_scan(q_dec), to_scan(k_dec), to_scan(g_last)))
    o = jnp.moveaxis(o, 0, 2).reshape(B, H, T, dv)
    return o, S_fin


def gdn_prepare(qkv, a, b, conv_w, a_log, dt_bias, rope):
    B, T, _ = qkv.shape
    qkv = jax.nn.silu(dwconv(qkv, conv_w)).astype(jnp.float32)
    q, k, v = jnp.split(qkv, 3, axis=-1)
    heads = lambda t: t.reshape(B, T, DN_HEADS, DN_HEAD_DIM).transpose(0, 2, 1, 3)
    q = l2norm(heads(q))
    k = l2norm(heads(k))
    v = heads(v)
    if rope is not None:
        q = apply_rope(q, *rope)
        k = apply_rope(k, *rope)
    a = a.astype(jnp.float32).reshape(B, T, 2, DN_HEADS).transpose(2, 0, 3, 1)
    b = b.astype(jnp.float32).reshape(B, T, 2, DN_HEADS).transpose(2, 0, 3, 1)
    g = -jnp.exp(a_log.astype(jnp.float32))[:, None, :, None] * jax.nn.softplus(a + dt_bias.astype(jnp.float32)[:, None, :, None])
    beta = jax.nn.sigmoid(b)
    return q, k, v, g, beta


def gdn_bidir(q, k, v, g, beta, S0f, S0b):
    o_f, S_f = chunk_gated_delta(q, k, v, g[0], beta[0], S0f)
    flip = lambda t: jnp.flip(t, axis=2)
    o_b, S_b = chunk_gated_delta(flip(q), flip(k), flip(v), flip(g[1]), flip(beta[1]), S0b)
    return o_f + flip(o_b), S_f, S_b


def gdn_output(o, z, norm_g):
    B, H, T, dv = o.shape
    o = o.transpose(0, 2, 1, 3)
    o = o * lax.rsqrt(jnp.mean(o * o, axis=-1, keepdims=True) + EPS) * norm_g.astype(jnp.float32)
    o = o * jax.nn.silu(z.astype(jnp.float32)).reshape(B, T, H, dv)
    return o.reshape(B, T, H * dv).astype(z.dtype)


def short_conv(sc_in, conv_w):
    bg, cg, hh = jnp.split(sc_in, 3, axis=-1)
    return bg * dwconv(cg * hh, conv_w)


def neighbourhood_attention(q, k, v, kc, vc, rpb):
    B, T, H, dh = q.shape
    rows = T // GRID_W
    wr = min(NA_WIN_R, rows)
    to_grid = lambda t: t.reshape(B, rows, GRID_W, H, dh).transpose(0, 3, 1, 2, 4)
    qg, kg, vg = to_grid(q), to_grid(k), to_grid(v)
    ncb = GRID_W // NA_QBLK_C
    qcol = np.arange(GRID_W).reshape(ncb, NA_QBLK_C)
    kstart = np.clip(np.arange(ncb) * NA_QBLK_C - NA_WIN_C // 2, 0, GRID_W - NA_KBLK_C)
    kcol = kstart[:, None] + np.arange(NA_KBLK_C)
    cstart = np.clip(qcol - NA_WIN_C // 2, 0, GRID_W - NA_WIN_C)
    col_ok = (kcol[:, None, :] >= cstart[..., None]) & (kcol[:, None, :] < cstart[..., None] + NA_WIN_C)
    col_idx = np.clip(kcol[:, None, :] - qcol[:, :, None] + NA_WIN_C - 1, 0, 2 * NA_WIN_C - 2)
    nl = wr * NA_KBLK_C
    mask = jnp.asarray(np.broadcast_to(col_ok[:, :, None, :], (ncb, NA_QBLK_C, wr, NA_KBLK_C)).reshape(ncb, NA_QBLK_C, nl))
    rpb_c = rpb[:, :, col_idx]
    scale = dh ** -0.5

    def row_block(r):
        rs = jnp.clip(r - wr // 2, 0, rows - wr)
        kr = lax.dynamic_slice_in_dim(kg, rs, wr, axis=2)[:, :, :, kcol]
        vr = lax.dynamic_slice_in_dim(vg, rs, wr, axis=2)[:, :, :, kcol]
        kr = kr.transpose(0, 1, 3, 2, 4, 5).reshape(B, H, ncb, nl, dh)
        vr = vr.transpose(0, 1, 3, 2, 4, 5).reshape(B, H, ncb, nl, dh)
        qr = lax.dynamic_index_in_dim(qg, r, axis=2, keepdims=False).reshape(B, H, ncb, NA_QBLK_C, dh)
        ridx = rs + jnp.arange(wr) - r + NA_WIN_R - 1
        bias = rpb_c[:, ridx].transpose(0, 2, 3, 1, 4).reshape(H, ncb, NA_QBLK_C, nl).astype(jnp.float32)
        s_loc = jnp.einsum('bhnqd,bhnkd->bhnqk', qr, kr).astype(jnp.float32) * scale + bias
        s_loc = jnp.where(mask, s_loc, -1e30)
        s_ctx = jnp.einsum('bhnqd,bhcd->bhnqc', qr, kc).astype(jnp.float32) * scale
        p = jax.nn.softmax(jnp.concatenate([s_loc, s_ctx], axis=-1), axis=-1).astype(v.dtype)
        o = jnp.einsum('bhnqk,bhnkd->bhnqd', p[..., :nl], vr) + jnp.einsum('bhnqc,bhcd->bhnqd', p[..., nl:], vc)
        return o.reshape(B, H, GRID_W, dh)

    o = lax.map(row_block, jnp.arange(rows))
    return o.transpose(1, 0, 3, 2, 4).reshape(B, T, H * dh)


def context_attention(qc, kc, vc):
    B, H, L, dh = qc.shape
    s = jnp.einsum('bhqd,bhkd->bhqk', qc, kc).astype(jnp.float32) * dh ** -0.5
    p = jax.nn.softmax(s, axis=-1).astype(vc.dtype)
    return jnp.einsum('bhqk,bhkd->bhqd', p, vc).transpose(0, 2, 1, 3).reshape(B, L, H * dh)


def merge(y_a, y_b, y_c, gates, w_pa, w_pb, w_pc, w_o):
    g_a, g_b, g_c = jnp.split(jax.nn.sigmoid(gates), N_BRANCH, axis=-1)
    return (g_a * (y_a @ w_pa) + g_b * (y_b @ w_pb) + g_c * (y_c @ w_pc)) @ w_o


def token_mixing(h, hc, w_in, dn_conv_w, dn_a_log, dn_dt_bias, dn_norm_g, sc_conv_w, na_rpb, w_pa, w_pb, w_pc, w_o, rope, need_ctx):
    B, T, _ = h.shape
    L = hc.shape[1]
    qkv, z, a, b, sc_in, na_in, gates = split_cols(h @ w_in)
    qkv_c, z_c, a_c, b_c, sc_in_c, na_in_c, gates_c = split_cols(hc @ w_in)
    cq, ck, cv, cg, cb = gdn_prepare(qkv_c, a_c, b_c, dn_conv_w, dn_a_log, dn_dt_bias, None)
    S0 = jnp.zeros((B, DN_HEADS, DN_HEAD_DIM, DN_HEAD_DIM), jnp.float32)
    o_c, S_f, S_b = gdn_bidir(cq, ck, cv, cg, cb, S0, S0)
    lq, lk, lv, lg, lb = gdn_prepare(qkv, a, b, dn_conv_w, dn_a_log, dn_dt_bias, rope)
    o_l, _, _ = gdn_bidir(lq, lk, lv, lg, lb, S_f, S_b)
    y_a = gdn_output(o_l, z, dn_norm_g)
    y_b = short_conv(sc_in, sc_conv_w)
    nq, nk, nv = [t.reshape(B, T, NA_HEADS, NA_HEAD_DIM) for t in jnp.split(na_in, 3, axis=-1)]
    nqc, nkc, nvc = [t.reshape(B, L, NA_HEADS, NA_HEAD_DIM).transpose(0, 2, 1, 3) for t in jnp.split(na_in_c, 3, axis=-1)]
    y_c = neighbourhood_attention(nq, nk, nv, nkc, nvc, na_rpb)
    y = merge(y_a, y_b, y_c, gates, w_pa, w_pb, w_pc, w_o)
    if not need_ctx:
        return y, None
    yc = merge(gdn_output(o_c, z_c, dn_norm_g), short_conv(sc_in_c, sc_conv_w), context_attention(nqc, nkc, nvc), gates_c, w_pa, w_pb, w_pc, w_o)
    return y, yc


def conv_ffn(h, w_up, conv_w, conv_b, w_down):
    u = dwconv(h @ w_up, conv_w) + conv_b
    a, b = jnp.split(u, 2, axis=-1)
    return (jax.nn.silu(a) * b) @ w_down


def setup_inputs(seed: int = 0) -> dict:
    key = jax.random.key(seed)
    ks = jax.random.split(key, 24)
    f32 = jnp.float32
    nrm = lambda k, shape, s: jax.random.normal(k, shape, f32) * s
    dt = jnp.exp(jax.random.uniform(ks[11], (DEPTH, 2, DN_HEADS), f32, math.log(1e-3), math.log(1e-1)))
    return {
        'x': nrm(ks[0], (BATCH, SEQ, D_MODEL), 1.0),
        'c': nrm(ks[1], (BATCH, D_MODEL), 1.0),
        'ctx': nrm(ks[2], (BATCH, CTX_LEN, D_MODEL), 1.0),
        'c_ctx': nrm(ks[3], (D_MODEL,), 1.0),
        'norm1_g': 1.0 + nrm(ks[4], (DEPTH, D_MODEL), 0.02),
        'norm2_g': 1.0 + nrm(ks[5], (DEPTH, D_MODEL), 0.02),
        'w_ada': nrm(ks[6], (DEPTH, D_MODEL, 6 * D_MODEL), 0.5 * D_MODEL ** -0.5),
        'b_ada': nrm(ks[7], (DEPTH, 6 * D_MODEL), 0.01),
        'w_in': nrm(ks[8], (DEPTH, D_MODEL, N_IN), D_MODEL ** -0.5),
        'dn_conv_w': nrm(ks[9], (DEPTH, DN_CONV, 3 * DN_WIDTH), DN_CONV ** -0.5),
        'dn_a_log': jnp.log(jax.random.uniform(ks[10], (DEPTH, 2, DN_HEADS), f32, 1.0, 16.0)),
        'dn_dt_bias': dt + jnp.log(-jnp.expm1(-dt)),
        'dn_norm_g': 1.0 + nrm(ks[12], (DEPTH, DN_HEAD_DIM), 0.02),
        'sc_conv_w': nrm(ks[13], (DEPTH, SC_CONV, SC_WIDTH), SC_CONV ** -0.5),
        'na_rpb': nrm(ks[14], (DEPTH, NA_HEADS, 2 * NA_WIN_R - 1, 2 * NA_WIN_C - 1), 0.1),
        'w_pa': nrm(ks[15], (DEPTH, DN_WIDTH, D_MODEL), DN_WIDTH ** -0.5),
        'w_pb': nrm(ks[16], (DEPTH, SC_WIDTH, D_MODEL), SC_WIDTH ** -0.5),
        'w_pc': nrm(ks[17], (DEPTH, NA_WIDTH, D_MODEL), NA_WIDTH ** -0.5),
        'w_o': nrm(ks[18], (DEPTH, D_MODEL, D_MODEL), D_MODEL ** -0.5),
        'w_up': nrm(ks[19], (DEPTH, D_MODEL, 2 * D_FF), D_MODEL ** -0.5),
        'ffn_conv_w': nrm(ks[20], (DEPTH, FFN_CONV, 2 * D_FF), FFN_CONV ** -0.5),
        'ffn_conv_b': nrm(ks[21], (DEPTH, 2 * D_FF), 0.01),
        'w_down': nrm(ks[22], (DEPTH, D_FF, D_MODEL), D_FF ** -0.5),
        'final_norm_g': 1.0 + nrm(ks[23], (D_MODEL,), 0.02),
    }


def reference(x, c, ctx, c_ctx, norm1_g, norm2_g, w_ada, b_ada, w_in, dn_conv_w, dn_a_log, dn_dt_bias, dn_norm_g, sc_conv_w, na_rpb, w_pa, w_pb, w_pc, w_o, w_up, ffn_conv_w, ffn_conv_b, w_down, final_norm_g):
    T = x.shape[1]
    rope = axial_rope(T, DN_HEAD_DIM)
    xc = ctx
    for l in range(DEPTH):
        need_ctx = l < DEPTH - 1
        mod = (jax.nn.silu(c) @ w_ada[l] + b_ada[l])[:, None, :]
        mod_c = jax.nn.silu(c_ctx) @ w_ada[l] + b_ada[l]
        sh1, sc1, gt1, sh2, sc2, gt2 = jnp.split(mod, 6, axis=-1)
        sh1c, sc1c, gt1c, sh2c, sc2c, gt2c = jnp.split(mod_c, 6, axis=-1)
        h = rms_norm(x, norm1_g[l]) * (1.0 + sc1) + sh1
        hc = rms_norm(xc, norm1_g[l]) * (1.0 + sc1c) + sh1c
        y, yc = token_mixing(h, hc, w_in[l], dn_conv_w[l], dn_a_log[l], dn_dt_bias[l], dn_norm_g[l], sc_conv_w[l], na_rpb[l], w_pa[l], w_pb[l], w_pc[l], w_o[l], rope, need_ctx)
        x = x + gt1 * y
        h2 = rms_norm(x, norm2_g[l]) * (1.0 + sc2) + sh2
        x = x + gt2 * conv_ffn(h2, w_up[l], ffn_conv_w[l], ffn_conv_b[l], w_down[l])
        if need_ctx:
            xc = xc + gt1c * yc
            h2c = rms_norm(xc, norm2_g[l]) * (1.0 + sc2c) + sh2c
            xc = xc + gt2c * conv_ffn(h2c, w_up[l], ffn_conv_w[l], ffn_conv_b[l], w_down[l])
    return rms_norm(x, final_norm_g)
```

```python
import contextlib
import numpy as np
import concourse.bass as bass
import concourse.mybir as mybir
from concourse.alu_op_type import AluOpType as ALU
from concourse.bass_utils import run_bass_kernel_spmd

F32 = mybir.dt.float32
BF16 = mybir.dt.bfloat16
AF = mybir.ActivationFunctionType

D = 1024
NCH = 8
DEPTH = 4
SEQ = 2048
LCTX = 256
NT = SEQ + LCTX
GRID_W = 64
NB_CORE = 2
DFF = 2816
NFF = 22
N_IN = 8208
EPS = 1e-6
SEM_EPOCH = 30000
C_QKV = 0
C_Z = 1536
C_A = 2048
C_B = 2056
C_SC = 2064
C_NA = 3600
C_G = 5136


class Res:
    __slots__ = ("w", "r")

    def __init__(self):
        self.w = None
        self.r = []


class Prog:
    def __init__(self, nc, n_dma_sems=32):
        self.nc = nc
        self.es = contextlib.ExitStack()
        self.eng = {"pe": nc.tensor, "dve": nc.vector, "act": nc.scalar, "pool": nc.gpsimd, "sp": nc.sync}
        self.semid = {}
        self.cnt = {}
        self.nsem = 0
        self.sems = {}
        for e in self.eng:
            self._new_epoch(e)
        self.waited = {e: {} for e in self.eng}
        self.dma_sems = []
        for i in range(n_dma_sems):
            self.dma_sems.append([self._alloc_sem("dma%d" % i), 0, None])
        self.dma_rr = 0
        self.ninst = 0
        self.res = {}

    def R(self, *key):
        r = self.res.get(key)
        if r is None:
            r = self.res[key] = Res()
        return r

    def _alloc_sem(self, name):
        h = self.es.enter_context(self.nc.semaphore(name))
        sid = self.nsem
        self.nsem += 1
        self.sems[sid] = h
        return sid

    def _new_epoch(self, e):
        self.semid[e] = self._alloc_sem("s_%s_%d" % (e, self.nsem))
        self.cnt[e] = 0

    def sbuf(self, name, shape, dt):
        self.nalloc = getattr(self, "nalloc", 0) + 1
        st = self.scopes[-1] if getattr(self, "scopes", None) else self.es
        return st.enter_context(self.nc.sbuf_tensor("sb_%s_%d" % (name, self.nalloc), list(shape), dt))

    @contextlib.contextmanager
    def scope(self):
        if not hasattr(self, "scopes"):
            self.scopes = []
        st = contextlib.ExitStack()
        self.scopes.append(st)
        try:
            yield
        finally:
            self.barrier()
            self.scopes.pop()
            st.close()

    def barrier(self):
        toks = [(self.semid[o], self.cnt[o]) for o in self.eng if self.cnt[o] > 0]
        toks += [s[2] for s in self.dma_sems if s[2] is not None]
        for e in self.eng:
            for t in toks:
                if t[0] != self.semid[e]:
                    self._wait(e, t)

    def psum(self, name, shape, dt):
        return self.es.enter_context(self.nc.psum_tensor("pp_" + name, list(shape), dt))

    def _wait(self, e, tok):
        if tok is None:
            return
        sid, val = tok
        w = self.waited[e]
        if w.get(sid, 0) >= val:
            return
        w[sid] = val
        self.eng[e].wait_ge(self.sems[sid], val)

    def _deps(self, e, reads, writes, is_dma=False):
        own = self.semid[e]
        skip_own = (e == "pe") and not is_dma
        for r in reads:
            t = r.w
            if t is not None and not (skip_own and t[0] == own):
                self._wait(e, t)
        for wr in writes:
            t = wr.w
            if t is not None and not (skip_own and t[0] == own):
                self._wait(e, t)
            for t in wr.r:
                if not (skip_own and t[0] == own):
                    self._wait(e, t)

    def _commit(self, tok, reads, writes):
        for r in reads:
            r.r.append(tok)
            if len(r.r) > 12:
                d = {}
                for s, v in r.r:
                    if d.get(s, 0) < v:
                        d[s] = v
                r.r = list(d.items())
        for wr in writes:
            wr.w = tok
            wr.r = []

    def op(self, e, fn, reads=(), writes=()):
        self._deps(e, reads, writes)
        if self.cnt[e] >= SEM_EPOCH:
            self._new_epoch(e)
        ins = fn(self.eng[e])
        self.cnt[e] += 1
        tok = (self.semid[e], self.cnt[e])
        ins.then_inc(self.sems[tok[0]], 1)
        self._commit(tok, reads, writes)
        self.ninst += 1
        return tok

    def dma(self, e, out, in_, reads=(), writes=(), **kw):
        slot = self.dma_sems[self.dma_rr]
        self.dma_rr = (self.dma_rr + 1) % len(self.dma_sems)
        if slot[2] is not None:
            self._wait(e, slot[2])
        self._deps(e, reads, writes, is_dma=True)
        slot[1] += 16
        tok = (slot[0], slot[1])
        slot[2] = tok
        self.eng[e].dma_start(out=out, in_=in_, **kw).then_inc(self.sems[slot[0]], 16)
        self._commit(tok, reads, writes)
        self.ninst += 1
        return tok

    def wait_all(self, e, resources):
        for r in resources:
            self._wait(e, r.w)
            for t in r.r:
                self._wait(e, t)

    def close(self):
        self.es.close()


def token_tiles(with_ctx=True, tile=512):
    tl = [(0, LCTX)] if with_ctx else []
    t = LCTX
    while t < NT:
        tl.append((t, min(tile, NT - t)))
        t += tile
    return tl


class Builder:
    def __init__(self, n_layers=DEPTH, nb=NB_CORE, stages=("mix", "gdn", "na", "ffn"), dbg=()):
        self.n_layers = n_layers
        self.nb = nb
        self.stages = stages
        self.dbg = dbg
        nc = self.nc = bass.Bass("TRN2", target_bir_lowering=False)
        self.P = Prog(nc)
        self.inp = {}
        self.outs = {}
        self.dbg_res = []
        self.declare_io()
        self.alloc()
        self.emit()
        self.P.close()

    def din(self, name, shape, dt=F32):
        self.inp[name] = self.nc.dram_tensor(name, list(shape), dt, kind="ExternalInput").ap()
        return self.inp[name]

    def dout(self, name, shape, dt=F32):
        self.outs[name] = self.nc.dram_tensor(name, list(shape), dt, kind="ExternalOutput").ap()
        return self.outs[name]

    def declare_io(self):
        nb = self.nb
        self.din("x", [nb, SEQ, D])
        self.din("ctx", [nb, LCTX, D])
        self.din("cvec", [nb + 1, D])
        self.din("norm1_g", [DEPTH + 1, D])
        self.din("norm2_g", [DEPTH, D])
        self.din("w_ada", [DEPTH, D, 6 * D])
        self.din("b_ada", [DEPTH, 6 * D])
        self.din("w_in", [DEPTH, D, N_IN])
        self.din("dn_conv_w", [DEPTH, 3, 1536])
        self.din("dn_a_log", [DEPTH, 8])
        self.din("dn_dt_bias", [DEPTH, 8])
        self.din("dn_norm_g", [DEPTH, 128])
        self.din("sc_conv_w", [DEPTH, 3, 512])
        self.din("w_pa", [DEPTH, 512, D])
        self.din("w_pb", [DEPTH, 512, D])
        self.din("w_pc", [DEPTH, 512, D])
        self.din("w_o", [DEPTH, D, D])
        self.din("w_up", [DEPTH, D, 2 * DFF])
        self.din("ffn_conv_w", [DEPTH, 4, 2 * DFF])
        self.din("w_down", [DEPTH, DFF, D])
        self.din("ident", [128, 128])
        self.din("na_rpb", [DEPTH, 128, 8, 14, 64])
        self.din("na_mask", [128, 14, 64])
        self.din("gdn_masks", [8, 128, 128])
        self.din("rope_cos", [128, SEQ])
        self.din("rope_sin", [128, SEQ])
        self.din("sperm", [128, 128])
        self.dout("out", [nb, SEQ, D])

    def alloc(self):
        P = self.P
        nb = self.nb
        self.xT = P.sbuf("xT", [128, NCH, NT], F32)
        self.ident = P.sbuf("ident", [128, 128], F32)
        self.identb = P.sbuf("identb", [128, 128], BF16)
        self.onesb = P.sbuf("onesb", [128, 128], BF16)
        self.eps = P.sbuf("eps", [128, 1], F32)
        self.ps = [P.psum("ps%d" % i, [128, 512], F32) for i in range(8)]
        self.ps_i = 0
        self.ps_rot = {}
        for nm in ("QD", "KD", "VD", "ZD"):
            setattr(self, nm, self.nc.dram_tensor("scr_" + nm, [18, 128, 4, 128], BF16, kind="Internal").ap())
        self.PA = self.nc.dram_tensor("scr_PA", [NCH, 128, NT], F32, kind="Internal").ap()
        self.PC = self.nc.dram_tensor("scr_PC", [NCH, 128, NT], F32, kind="Internal").ap()
        L = self.n_layers
        self.g1t = P.sbuf("g1t", [128, NCH, DEPTH + 1], F32)
        self.g2t = P.sbuf("g2t", [128, NCH, DEPTH], F32)
        self.fcwt = P.sbuf("fcwt", [128, L, 44, 4], F32)
        self.mod = P.sbuf("mod", [128, L, 48, nb + 1], F32)
        self.s1 = P.sbuf("s1", [128, L, nb + 1, NCH], F32)
        self.s2 = P.sbuf("s2", [128, L, nb + 1, NCH], F32)
        self.sq = P.sbuf("sq", [128, NCH, 512], BF16)
        self.rstd = P.sbuf("rstd", [128, 512], F32)
        self.tmp = [P.sbuf("tmp%d" % i, [128, 512], F32) for i in range(2)]
        self.tmp_i = 0

    def alloc_rows(self):
        self.rows = [self.P.sbuf("rows%d" % i, [128, D], F32) for i in range(2)]

    def next_ps(self):
        i = self.ps_i
        self.ps_i = (i + 1) % 8
        return self.ps[i], self.P.R("ps", i)

    def next_tmp(self):
        i = self.tmp_i
        self.tmp_i = (i + 1) % len(self.tmp)
        return self.tmp[i], self.P.R("tmp", i)

    def load_rows_T(self, src_rows, nrows, nchunks, dst_fn, tag):
        P = self.P
        st = self.rows[0] if nchunks * 128 <= D else None
        done = 0
        while done < nchunks:
            ncb = min(8, nchunks - done)
            buf, rb = self.rows[0], P.R("rows", 0)
            P.dma("sp", buf[0:nrows, 0:ncb * 128], src_rows[:, done * 128:(done + ncb) * 128], writes=[rb])
            ps, rp = self.next_ps()
            for c in range(ncb):
                P.op("pe", lambda e, c=c: e.transpose(ps[:, c * nrows:(c + 1) * nrows], buf[0:nrows, c * 128:(c + 1) * 128],
                                                      self.ident[0:nrows, 0:nrows]),
                     reads=[rb, P.R("ident")], writes=[rp])
            dst_fn(ps[:, 0:ncb * nrows].rearrange("p (c r) -> p c r", r=nrows), done, ncb, rp)
            done += ncb

    def emit(self):
        P = self.P
        nb = self.nb
        L = self.n_layers
        I = self.inp
        P.dma("sp", self.ident[:], I["ident"], writes=[P.R("ident")])
        P.op("dve", lambda e: e.tensor_copy(out=self.identb[:], in_=self.ident[:]), reads=[P.R("ident")], writes=[P.R("identb")])
        P.op("dve", lambda e: e.memset(self.onesb[:], 1.0), writes=[P.R("onesb")])
        P.op("dve", lambda e: e.memset(self.eps[:], EPS), writes=[P.R("eps")])
        with P.scope():
            self.alloc_rows()
            self.emit_params()
            self.dump("mod", self.mod[:], [128, L, 48, nb + 1], [P.R("params")])
            self.dump("s2", self.s2[:], [128, L, nb + 1, NCH], [P.R("params")])
            self.dump("fcwt", self.fcwt[:], [128, L, 44, 4], [P.R("params")])
            self.dump("scT", self.scT[:], [128, NCH, nb + 1], [P.R("scT")])
        for b in range(nb):
            with P.scope():
                self.alloc_rows()
                self.emit_load_x(b)
            if b == 0:
                self.dump("xT", self.xT[:], [128, NCH, NT], [r for c in range(NCH) for r in self.xres(c, 0, NT)])
            for l in range(L):
                if "mix" in self.stages:
                    with P.scope():
                        self.emit_mixer(l, b)
                if b == 0 and l == 0:
                    self.dump("xmid", self.xT[:], [128, NCH, NT], [r for c in range(NCH) for r in self.xres(c, 0, NT)])
                if "ffn" in self.stages:
                    with P.scope():
                        _alloc_ffn(self)
                        self.emit_ffn(l, b)
            with P.scope():
                self.alloc_rows()
                self.fin_buf = P.sbuf("fin", [128, NCH, 512], F32)
                self.emit_final(b)
        P.wait_all("sp", [P.R("out", b, j) for b in range(nb) for j in range(SEQ // 128)] + self.dbg_res)

    def dump(self, name, ap, shape, reads, dt=F32):
        if name not in self.dbg:
            return
        o = self.dout("dbg_" + name, shape, dt)
        self.P.dma("sp", o, ap, reads=reads, writes=[self.P.R("dbgout", name)])
        self.dbg_res.append(self.P.R("dbgout", name))

    def emit_params(self):
        P = self.P
        nb = self.nb
        L = self.n_layers
        I = self.inp
        ncol = nb + 1
        self.scT = P.sbuf("scT", [128, NCH, ncol], F32)
        cs = self.rows[1]
        rcs = P.R("rows", 1)
        P.dma("sp", cs[0:ncol, :], I["cvec"], writes=[rcs])
        P.op("act", lambda e: e.activation(out=cs[0:ncol, :], in_=cs[0:ncol, :], func=AF.Silu), reads=[rcs], writes=[rcs])
        ps, rp = self.next_ps()
        for c in range(NCH):
            P.op("pe", lambda e, c=c: e.transpose(ps[:, c * ncol:(c + 1) * ncol], cs[0:ncol, c * 128:(c + 1) * 128],
                                                  self.ident[0:ncol, 0:ncol]), reads=[rcs, P.R("ident")], writes=[rp])
        P.op("dve", lambda e: e.tensor_copy(out=self.scT[:], in_=ps[:, 0:NCH * ncol].rearrange("p (c r) -> p c r", r=ncol)),
             reads=[rp], writes=[P.R("scT")])
        def mk(dst3):
            def f(psv, c0, ncb, rp):
                P.op("dve", lambda e: e.tensor_copy(out=dst3[:, c0:c0 + ncb, :], in_=psv), reads=[rp], writes=[P.R("params")])
            return f
        self.load_rows_T(I["norm1_g"], DEPTH + 1, NCH, mk(self.g1t), "g1")
        self.load_rows_T(I["norm2_g"], DEPTH, NCH, mk(self.g2t), "g2")
        for l in range(L):
            self.load_rows_T(I["ffn_conv_w"][l], 4, 44, mk(self.fcwt[:, l]), "fcw")
        wA = [P.sbuf("wA%d" % i, [128, NCH, 512], F32) for i in range(2)]
        bA = P.sbuf("bA", [1, 6 * D], F32)
        ones1 = P.sbuf("ones1", [1, 4], F32)
        P.op("dve", lambda e: e.memset(ones1[:], 1.0), writes=[P.R("ones1")])
        it = 0
        for l in range(L):
            P.dma("sp", bA[:], I["b_ada"][l:l + 1, :], writes=[P.R("bA")])
            ps, rp = self.next_ps()
            for cb in range(12):
                w = wA[it % 2]
                rw = P.R("wA", it % 2)
                it += 1
                P.dma("sp", w[:], I["w_ada"][l][:, cb * 512:(cb + 1) * 512].rearrange("(c p) n -> p c n", p=128), writes=[rw])
                for f in range(4):
                    j = cb * 4 + f
                    for k in range(NCH):
                        P.op("pe", lambda e, k=k, f=f, j=j, w=w: e.matmul(ps[:, j * ncol:(j + 1) * ncol], w[:, k, f * 128:(f + 1) * 128],
                                                                     self.scT[:, k, :], start=(k == 0), stop=False),
                             reads=[rw, P.R("scT")], writes=[rp])
                    P.op("pe", lambda e, j=j: e.matmul(ps[:, j * ncol:(j + 1) * ncol], bA[0:1, j * 128:(j + 1) * 128],
                                                       ones1[0:1, 0:ncol], start=False, stop=True),
                         reads=[P.R("bA"), P.R("ones1")], writes=[rp])
            P.op("dve", lambda e, l=l: e.tensor_copy(out=self.mod[:, l], in_=ps[:, 0:48 * ncol].rearrange("p (j r) -> p j r", r=ncol)),
                 reads=[rp], writes=[P.R("params")])
            for col in range(ncol):
                P.op("dve", lambda e, l=l, col=col: e.scalar_tensor_tensor(
                    out=self.s1[:, l, col, :], in0=self.mod[:, l, 8:16, col], scalar=1.0, in1=self.g1t[:, :, l],
                    op0=ALU.add, op1=ALU.mult), reads=[P.R("params")], writes=[P.R("params")])
                P.op("dve", lambda e, l=l, col=col: e.scalar_tensor_tensor(
                    out=self.s2[:, l, col, :], in0=self.mod[:, l, 32:40, col], scalar=1.0, in1=self.g2t[:, :, l],
                    op0=ALU.add, op1=ALU.mult), reads=[P.R("params")], writes=[P.R("params")])

    def emit_load_x(self, b):
        P = self.P
        I = self.inp
        for j in range(NT // 128):
            buf = self.rows[j % 2]
            rb = P.R("rows", j % 2)
            if j < 2:
                src = I["ctx"][b, j * 128:(j + 1) * 128, :]
            else:
                src = I["x"][b, (j - 2) * 128:(j - 1) * 128, :]
            P.dma("sp", buf[:], src, writes=[rb])
            for half in range(2):
                ps, rp = self.next_ps()
                for c in range(4):
                    cc = half * 4 + c
                    P.op("pe", lambda e, c=c, cc=cc: e.transpose(ps[:, c * 128:(c + 1) * 128], buf[:, cc * 128:(cc + 1) * 128],
                                                                 self.ident[:]), reads=[rb, P.R("ident")], writes=[rp])
                eng = "dve" if half == 0 else "act"
                dst = self.xT[:, half * 4:(half + 1) * 4, j * 128:(j + 1) * 128]
                src_ps = ps[:, :].rearrange("p (c t) -> p c t", t=128)
                wr = [self.xres(c, j * 128, 128) for c in range(half * 4, half * 4 + 4)]
                wr = [r for rr in wr for r in rr]
                if eng == "dve":
                    P.op("dve", lambda e: e.tensor_copy(out=dst, in_=src_ps), reads=[rp], writes=wr)
                else:
                    P.op("act", lambda e: e.copy(out=dst, in_=src_ps), reads=[rp], writes=wr)

    def xres(self, c, t0, n):
        return [self.P.R("xT", c, j) for j in range(t0 // 128, (t0 + n + 127) // 128)]

    def emit_norm(self, t0, n, scale_ap_fn, shift_ap_fn, dst, rdst, dst_off=0):
        P = self.P
        xr = [r for c in range(NCH) for r in self.xres(c, t0, n)]
        P.op("act", lambda e: e.activation(out=self.sq[:, :, 0:n], in_=self.xT[:, :, t0:t0 + n], func=AF.Square),
             reads=xr, writes=[P.R("sq")])
        ps, rp = self.next_ps()
        for c in range(NCH):
            P.op("pe", lambda e, c=c: e.matmul(ps[:, 0:n], self.onesb[:], self.sq[:, c, 0:n], start=(c == 0), stop=(c == NCH - 1)),
                 reads=[P.R("sq"), P.R("onesb")], writes=[rp])
        P.op("act", lambda e: e.activation(out=self.rstd[:, 0:n], in_=ps[:, 0:n], func=AF.Sqrt, bias=self.eps[:], scale=1.0 / D),
             reads=[rp, P.R("eps")], writes=[P.R("rstd")])
        P.op("dve", lambda e: e.reciprocal(out=self.rstd[:, 0:n], in_=self.rstd[:, 0:n]), reads=[P.R("rstd")], writes=[P.R("rstd")])
        for c in range(NCH):
            tmp, rt = self.next_tmp()
            P.op("dve", lambda e, c=c, tmp=tmp: e.scalar_tensor_tensor(
                out=tmp[:, 0:n], in0=self.xT[:, c, t0:t0 + n], scalar=scale_ap_fn(c), in1=self.rstd[:, 0:n],
                op0=ALU.mult, op1=ALU.mult), reads=self.xres(c, t0, n) + [P.R("rstd"), P.R("params")], writes=[rt])
            sh = shift_ap_fn(c)
            if sh is None:
                P.op("act", lambda e, c=c, tmp=tmp: e.copy(out=dst[:, c, dst_off:dst_off + n], in_=tmp[:, 0:n]),
                     reads=[rt], writes=[rdst])
            else:
                P.op("act", lambda e, c=c, tmp=tmp, sh=sh: e.activation(out=dst[:, c, dst_off:dst_off + n], in_=tmp[:, 0:n],
                                                                      func=AF.Identity, bias=sh, scale=1.0),
                     reads=[rt, P.R("params")], writes=[rdst])

    def emit_final(self, b):
        P = self.P
        O = self.outs["out"]
        fin = self.fin_buf
        for ti, (t0, n) in enumerate(token_tiles(with_ctx=False)):
            rf = P.R("fin")
            self.emit_norm_f32(t0, n, fin, rf)
            if b == 0 and ti == 0:
                self.dump("fin", fin[:], [128, NCH, 512], [rf])
                self.dump("rstd", self.rstd[:], [128, 512], [P.R("rstd")])
                self.dump("sq", self.sq[:], [128, NCH, 512], [P.R("sq")], BF16)
            for jj in range(n // 128):
                j = (t0 - LCTX) // 128 + jj
                buf = self.rows[j % 2]
                rb = P.R("rows", j % 2)
                for half in range(2):
                    ps, rp = self.next_ps()
                    for c in range(4):
                        cc = half * 4 + c
                        P.op("pe", lambda e, c=c, cc=cc, jj=jj: e.transpose(ps[:, c * 128:(c + 1) * 128],
                                                                             fin[:, cc, jj * 128:(jj + 1) * 128], self.ident[:]),
                             reads=[rf, P.R("ident")], writes=[rp])
                    if half == 0:
                        P.op("dve", lambda e, half=half: e.tensor_copy(out=buf[:, half * 512:(half + 1) * 512], in_=ps[:, :]),
                             reads=[rp], writes=[rb])
                    else:
                        P.op("act", lambda e, half=half: e.copy(out=buf[:, half * 512:(half + 1) * 512], in_=ps[:, :]),
                             reads=[rp], writes=[rb])
                P.dma("sp", O[b, j * 128:(j + 1) * 128, :], buf[:], reads=[rb], writes=[P.R("out", b, j)])

    def emit_norm_f32(self, t0, n, dst, rdst):
        P = self.P
        xr = [r for c in range(NCH) for r in self.xres(c, t0, n)]
        P.op("act", lambda e: e.activation(out=self.sq[:, :, 0:n], in_=self.xT[:, :, t0:t0 + n], func=AF.Square),
             reads=xr, writes=[P.R("sq")])
        ps, rp = self.next_ps()
        for c in range(NCH):
            P.op("pe", lambda e, c=c: e.matmul(ps[:, 0:n], self.onesb[:], self.sq[:, c, 0:n], start=(c == 0), stop=(c == NCH - 1)),
                 reads=[P.R("sq"), P.R("onesb")], writes=[rp])
        P.op("act", lambda e: e.activation(out=self.rstd[:, 0:n], in_=ps[:, 0:n], func=AF.Sqrt, bias=self.eps[:], scale=1.0 / D),
             reads=[rp, P.R("eps")], writes=[P.R("rstd")])
        P.op("dve", lambda e: e.reciprocal(out=self.rstd[:, 0:n], in_=self.rstd[:, 0:n]), reads=[P.R("rstd")], writes=[P.R("rstd")])
        for c in range(NCH):
            P.op("dve", lambda e, c=c: e.scalar_tensor_tensor(
                out=dst[:, c, 0:n], in0=self.xT[:, c, t0:t0 + n], scalar=self.g1t[:, c, DEPTH:DEPTH + 1], in1=self.rstd[:, 0:n],
                op0=ALU.mult, op1=ALU.mult), reads=self.xres(c, t0, n) + [P.R("rstd"), P.R("params")], writes=[rdst])

    def emit_h(self, l, b):
        P = self.P
        nb = self.nb
        self.hT = P.sbuf("hT", [128, NCH, NT], BF16)
        tl = token_tiles()
        self.hres = [P.R("hT", ti) for ti in range(len(tl))]
        for ti, (t0, n) in enumerate(tl):
            col = nb if t0 < LCTX else b
            self.emit_norm(t0, n, lambda c: self.s1[:, l, col, c:c + 1], lambda c: self.mod[:, l, c, col:col + 1],
                           self.hT, self.hres[ti], dst_off=t0)

    def emit_mixer(self, l, b):
        P = self.P
        if "gdn" in self.stages:
            with P.scope():
                self.gG = P.sbuf("gdnG", [128, 18, 8], F32)
                self.gBT = P.sbuf("gdnB", [128, 18, 8], F32)
                self.gNBT = P.sbuf("gdnNB", [128, 18, 8], F32)
                with P.scope():
                    self.emit_h(l, b)
                    self.emit_gdn_A(l, b)
                with P.scope():
                    self.emit_gdn_B(l, b)
        with P.scope():
            self.emit_h(l, b)
            if "na" in self.stages:
                with P.scope():
                    self.emit_na(l, b)
            with P.scope():
                self.emit_merge(l, b)

    def load_w(self, dst, rdst, src, eng="pool"):
        self.P.dma(eng, dst, src.rearrange("(c p) n -> p c n", p=128), writes=[rdst])

    def proj_to_dram(self, yT, ry, w_dram, dst_dram, key, tiles):
        P = self.P
        wp = [P.sbuf("wpj%d" % i, [128, 4, 128], BF16) for i in range(2)]
        st = [P.sbuf("pjst%d" % i, [128, 512], F32) for i in range(2)]
        it = 0
        for oc in range(NCH):
            w, rw = wp[oc % 2], P.R("wpj", key, oc % 2)
            self.load_w(w[:], rw, w_dram[:, oc * 128:(oc + 1) * 128])
            for (t0, n) in tiles:
                ps, rp = self.next_ps()
                for k in range(4):
                    P.op("pe", lambda e, k=k, w=w: e.matmul(ps[:, 0:n], w[:, k, :], yT[:, k, t0:t0 + n], start=(k == 0), stop=(k == 3)),
                         reads=[rw] + ry, writes=[rp])
                sb, rs = st[it % 2], P.R("pjst", key, it % 2)
                it += 1
                P.op("act", lambda e, sb=sb, ps=ps: e.copy(out=sb[:, 0:n], in_=ps[:, 0:n]), reads=[rp], writes=[rs])
                P.dma("sp", dst_dram[oc, :, t0:t0 + n], sb[:, 0:n], reads=[rs], writes=[P.R(key, oc, t0)])

    def ps_from(self, banks, key):
        i = self.ps_rot.get(key, 0)
        self.ps_rot[key] = (i + 1) % len(banks)
        bk = banks[i]
        return self.ps[bk], self.P.R("ps", bk)

    def emit_na(self, l, b):
        P = self.P
        I = self.inp
        nb = self.nb
        need_ctx = l < DEPTH - 1
        hall = self.hres
        ycT = P.sbuf("ycT", [128, 4, NT], BF16)
        rycs = [P.R("ycT", j) for j in range(4)]
        qT = P.sbuf("naq", [128, NT], BF16)
        kT = P.sbuf("nak", [128, NT], BF16)
        vE = P.sbuf("nave", [128, 18, 128], BF16)
        vO = P.sbuf("navo", [128, 15, 128], BF16)
        TAB = P.sbuf("natab", [128, 2, 14, 64], BF16)
        tst = P.sbuf("natst", [128, 14, 64], F32)
        nmask = P.sbuf("nanm", [128, 14, 64], F32)
        pT = [P.sbuf("napT%d" % i, [128, 512], BF16) for i in range(2)]
        rec = [P.sbuf("narec%d" % i, [128, 512], F32) for i in range(2)]
        wq = [P.sbuf("nawq%d" % i, [128, NCH, 3, 128], BF16) for i in range(2)]
        rq, rk, rvE, rvO, rtab = P.R("naq"), P.R("nak"), P.R("nave"), P.R("navo"), P.R("natab")
        P.dma("sp", nmask[:], I["na_mask"], writes=[P.R("nanm")])
        SB = [4, 5, 6, 7]
        AB = [0, 1, 2, 3]
        pti = 0
        for hp in range(4):
            w, rw = wq[hp % 2], P.R("nawq", hp % 2)
            for q in range(3):
                c0 = C_NA + q * 512 + hp * 128
                self.load_w(w[:, :, q, :], rw, I["w_in"][l][:, c0:c0 + 128])
            for hh in range(2):
                P.dma("sp", tst[:], I["na_rpb"][l, :, 2 * hp + hh], writes=[P.R("natst")])
                P.op("dve", lambda e, hh=hh: e.tensor_tensor(out=TAB[:, hh], in0=tst[:], in1=nmask[:], op=ALU.add),
                     reads=[P.R("natst"), P.R("nanm")], writes=[rtab])
            for (t0, n) in token_tiles():
                for q, dst, rd in ((0, qT, rq), (1, kT, rk)):
                    if q == 0 and t0 < LCTX and not need_ctx:
                        continue
                    ps, rp = self.ps_from(SB, "s")
                    for k in range(NCH):
                        P.op("pe", lambda e, k=k, q=q, ps=ps: e.matmul(ps[:, 0:n], w[:, k, q, :], self.hT[:, k, t0:t0 + n],
                                                                      start=(k == 0), stop=(k == NCH - 1)), reads=[rw] + hall, writes=[rp])
                    if q == 0:
                        P.op("act", lambda e, ps=ps: e.activation(out=qT[:, t0:t0 + n], in_=ps[:, 0:n], func=AF.Copy, scale=0.125),
                             reads=[rp], writes=[rq])
                    else:
                        P.op("dve", lambda e, ps=ps: e.tensor_copy(out=kT[:, t0:t0 + n], in_=ps[:, 0:n]), reads=[rp], writes=[rk])
            for grp, (dstv, rdv, ntile, tok0) in enumerate(((vE, rvE, 18, 0), (vO, rvO, 15, LCTX + 64))):
                for j0 in range(0, ntile, 4):
                    nj = min(4, ntile - j0)
                    ps, rp = self.ps_from(SB, "s")
                    for jj in range(nj):
                        ts = tok0 + (j0 + jj) * 128
                        for k in range(NCH):
                            P.op("pe", lambda e, k=k, jj=jj, ts=ts, ps=ps: e.matmul(ps[:, jj * 128:(jj + 1) * 128], self.hT[:, k, ts:ts + 128], w[:, k, 2, :],
                                                                                  start=(k == 0), stop=(k == NCH - 1)), reads=[rw] + hall, writes=[rp])
                    P.op("act", lambda e, ps=ps, j0=j0, nj=nj, dstv=dstv: e.copy(out=dstv[:, j0:j0 + nj, :],
                                                                                in_=ps[:, 0:nj * 128].rearrange("p (j f) -> p j f", f=128)),
                         reads=[rp], writes=[rdv])
            for g in range(4):
                psO, rpO = self.ps_from(AB, "a")
                psD, rpD = self.ps_from(AB, "a")
                for rr in range(8):
                    r = g * 8 + rr
                    rs = min(max(r - 4, 0), 24)
                    qt0 = LCTX + r * 64
                    for pi in range(2):
                        pr = slice(pi * 64, (pi + 1) * 64)
                        psS, rpS = self.ps_from(SB, "s")
                        vts = []
                        for i in range(4):
                            krow = rs + 2 * i
                            kt0 = LCTX + krow * 64
                            a = krow - r + 7
                            P.op("pe", lambda e, i=i, kt0=kt0, psS=psS: e.matmul(psS[:, i * 64:(i + 1) * 64], kT[pr, kt0:kt0 + 128], qT[pr, qt0:qt0 + 64],
                                                                                 start=True, stop=False), reads=[rk, rq], writes=[rpS])
                            P.op("pe", lambda e, i=i, a=a, psS=psS: e.matmul(psS[:, i * 64:(i + 1) * 64], self.identb[:], TAB[:, pi, a, :],
                                                                             start=False, stop=True), reads=[rtab, P.R("identb")], writes=[rpS])
                            if krow % 2 == 0:
                                vts.append((vE, rvE, 2 + krow // 2))
                            else:
                                vts.append((vO, rvO, (krow - 1) // 2))
                        for i in range(2):
                            P.op("pe", lambda e, i=i, psS=psS: e.matmul(psS[:, (4 + i) * 64:(5 + i) * 64], kT[pr, i * 128:(i + 1) * 128], qT[pr, qt0:qt0 + 64],
                                                                        start=True, stop=True), reads=[rk, rq], writes=[rpS])
                            vts.append((vE, rvE, i))
                        pt, rpt = pT[pti % 2], P.R("napT", pti % 2)
                        pti += 1
                        P.op("act", lambda e, psS=psS, pt=pt: e.activation(out=pt[:, 0:384], in_=psS[:, 0:384], func=AF.Exp), reads=[rpS], writes=[rpt])
                        for i, (vt, rvt, vj) in enumerate(vts):
                            P.op("pe", lambda e, i=i, vt=vt, vj=vj, pt=pt: e.matmul(psO[pr, rr * 64:(rr + 1) * 64], vt[:, vj, pr], pt[:, i * 64:(i + 1) * 64],
                                                                                  start=(i == 0), stop=(i == 5)), reads=[rvt, rpt], writes=[rpO])
                        for i in range(6):
                            P.op("pe", lambda e, i=i, pt=pt: e.matmul(psD[pr, rr * 64:(rr + 1) * 64], self.onesb[:, 0:64], pt[:, i * 64:(i + 1) * 64],
                                                                      start=(i == 0), stop=(i == 5)), reads=[P.R("onesb"), rpt], writes=[rpD])
                rc, rrc = rec[g % 2], P.R("narec", g % 2)
                P.op("dve", lambda e, rc=rc, psD=psD: e.reciprocal(out=rc[:], in_=psD[:]), reads=[rpD], writes=[rrc])
                t0 = LCTX + g * 512
                P.op("dve", lambda e, rc=rc, psO=psO, t0=t0: e.tensor_tensor(out=ycT[:, hp, t0:t0 + 512], in0=psO[:], in1=rc[:], op=ALU.mult),
                     reads=[rpO, rrc], writes=[rycs[hp]])
            if need_ctx:
                psO, rpO = self.ps_from(AB, "a")
                psD, rpD = self.ps_from(AB, "a")
                for pi in range(2):
                    pr = slice(pi * 64, (pi + 1) * 64)
                    psS, rpS = self.ps_from(SB, "s")
                    for i in range(2):
                        P.op("pe", lambda e, i=i, psS=psS: e.matmul(psS[:, i * 256:(i + 1) * 256], kT[pr, i * 128:(i + 1) * 128], qT[pr, 0:LCTX],
                                                                    start=True, stop=True), reads=[rk, rq], writes=[rpS])
                    pt, rpt = pT[pti % 2], P.R("napT", pti % 2)
                    pti += 1
                    P.op("act", lambda e, psS=psS, pt=pt: e.activation(out=pt[:], in_=psS[:], func=AF.Exp), reads=[rpS], writes=[rpt])
                    for i in range(2):
                        P.op("pe", lambda e, i=i, pt=pt: e.matmul(psO[pr, 0:LCTX], vE[:, i, pr], pt[:, i * 256:(i + 1) * 256],
                                                                  start=(i == 0), stop=(i == 1)), reads=[rvE, rpt], writes=[rpO])
                    for i in range(2):
                        P.op("pe", lambda e, i=i, pt=pt: e.matmul(psD[pr, 0:LCTX], self.onesb[:, 0:64], pt[:, i * 256:(i + 1) * 256],
                                                                  start=(i == 0), stop=(i == 1)), reads=[P.R("onesb"), rpt], writes=[rpD])
                rc, rrc = rec[0], P.R("narec", 0)
                P.op("dve", lambda e, rc=rc, psD=psD: e.reciprocal(out=rc[:, 0:LCTX], in_=psD[:, 0:LCTX]), reads=[rpD], writes=[rrc])
                P.op("dve", lambda e, rc=rc, psO=psO: e.tensor_tensor(out=ycT[:, hp, 0:LCTX], in0=psO[:, 0:LCTX], in1=rc[:, 0:LCTX], op=ALU.mult),
                     reads=[rpO, rrc], writes=[rycs[hp]])
        if l == 0 and b == 0:
            self.dump("ycT", ycT[:], [128, 4, NT], rycs, BF16)
        tiles = [t for t in token_tiles() if need_ctx or t[0] >= LCTX]
        self.proj_to_dram(ycT, rycs, I["w_pc"][l], self.PC, "PC", tiles)

    def emit_merge(self, l, b):
        P = self.P
        I = self.inp
        nb = self.nb
        need_ctx = l < DEPTH - 1
        tiles = [t for t in token_tiles() if need_ctx or t[0] >= LCTX]
        use_a = "gdn" in self.stages
        use_c = "na" in self.stages
        hall = self.hres
        ybT = P.sbuf("ybT", [128, 4, NT], BF16)
        rybs = [P.R("ybT", j) for j in range(4)]
        with P.scope():
            self.alloc_rows()
            PW = NT + 4
            pbuf = P.sbuf("pbuf", [128, PW], F32)
            ybuf = P.sbuf("ybuf", [128, PW], F32)
            bgb = P.sbuf("bgb", [128, PW], F32)
            wsc = [P.sbuf("wsc%d" % i, [128, NCH, 3, 128], BF16) for i in range(2)]
            scw = P.sbuf("scw", [128, 4, 3], F32)
            rpb_, ryb_, rbg_ = P.R("pbuf"), P.R("ybuf"), P.R("bgb")
            P.op("dve", lambda e: e.memset(pbuf[:], 0.0), writes=[rpb_])
            P.op("dve", lambda e: e.memset(bgb[:], 0.0), writes=[rbg_])

            def mk(psv, c0, ncb, rp):
                P.op("dve", lambda e: e.tensor_copy(out=scw[:, c0:c0 + ncb, :], in_=psv), reads=[rp], writes=[P.R("scw")])
            self.load_rows_T(I["sc_conv_w"][l], 3, 4, mk, "scw")
            off = lambda t: t + 1 if t < LCTX else t + 3
            for j in range(4):
                w, rw = wsc[j % 2], P.R("wsc", j % 2)
                for q in range(3):
                    c0 = C_SC + q * 512 + j * 128
                    self.load_w(w[:, :, q, :], rw, I["w_in"][l][:, c0:c0 + 128])
                for (t0, n) in tiles:
                    pss = []
                    for q in range(3):
                        ps, rp = self.next_ps()
                        for k in range(NCH):
                            P.op("pe", lambda e, k=k, q=q, ps=ps: e.matmul(ps[:, 0:n], w[:, k, q, :], self.hT[:, k, t0:t0 + n],
                                                                          start=(k == 0), stop=(k == NCH - 1)), reads=[rw] + hall, writes=[rp])
                        pss.append((ps, rp))
                    o = off(t0)
                    tmp, rt = self.next_tmp()
                    P.op("act", lambda e, tmp=tmp: e.copy(out=tmp[:, 0:n], in_=pss[2][0][:, 0:n]), reads=[pss[2][1]], writes=[rt])
                    P.op("dve", lambda e, tmp=tmp: e.tensor_tensor(out=pbuf[:, o:o + n], in0=pss[1][0][:, 0:n], in1=tmp[:, 0:n], op=ALU.mult),
                         reads=[pss[1][1], rt], writes=[rpb_])
                    P.op("act", lambda e: e.copy(out=bgb[:, o:o + n], in_=pss[0][0][:, 0:n]), reads=[pss[0][1]], writes=[rbg_])
                P.op("act", lambda e, j=j: e.activation(out=ybuf[:], in_=pbuf[:], func=AF.Copy, scale=scw[:, j, 1:2]),
                     reads=[rpb_, P.R("scw")], writes=[ryb_])
                P.op("dve", lambda e, j=j: e.scalar_tensor_tensor(out=ybuf[:, 1:PW], in0=pbuf[:, 0:PW - 1], scalar=scw[:, j, 0:1], in1=ybuf[:, 1:PW],
                                                             op0=ALU.mult, op1=ALU.add), reads=[rpb_, ryb_, P.R("scw")], writes=[ryb_])
                P.op("dve", lambda e, j=j: e.scalar_tensor_tensor(out=ybuf[:, 0:PW - 1], in0=pbuf[:, 1:PW], scalar=scw[:, j, 2:3], in1=ybuf[:, 0:PW - 1],
                                                             op0=ALU.mult, op1=ALU.add), reads=[rpb_, ryb_, P.R("scw")], writes=[ryb_])
                if need_ctx:
                    P.op("dve", lambda e, j=j: e.tensor_tensor(out=ybT[:, j, 0:LCTX], in0=ybuf[:, 1:1 + LCTX], in1=bgb[:, 1:1 + LCTX], op=ALU.mult),
                         reads=[ryb_, rbg_], writes=[rybs[j]])
                P.op("dve", lambda e, j=j: e.tensor_tensor(out=ybT[:, j, LCTX:NT], in0=ybuf[:, 3 + LCTX:3 + NT], in1=bgb[:, 3 + LCTX:3 + NT], op=ALU.mult),
                     reads=[ryb_, rbg_], writes=[rybs[j]])
            self.dump("ybT", ybT[:], [128, 4, NT], rybs, BF16) if (l == 0 and b == 0) else None
        preT = P.sbuf("preT", [128, NCH, NT], BF16)
        with P.scope():
            wg = [P.sbuf("wg%d" % i, [128, NCH, 3, 128], BF16) for i in range(2)]
            wpb = [P.sbuf("wpb%d" % i, [128, 4, 128], BF16) for i in range(2)]
            sig = [self.tmp[0], self.tmp[1], P.sbuf("sig2", [128, 512], F32)]
            pab = [P.sbuf("pab%d" % i, [128, 512], F32) for i in range(1)]
            pcb = [P.sbuf("pcb%d" % i, [128, 512], F32) for i in range(1)]
            acc2 = [P.sbuf("acc2_%d" % i, [128, 512], F32) for i in range(1)]
            it = 0
            sigr = [P.R("tmp", 0), P.R("tmp", 1), P.R("sig", 2)]
            for oc in range(NCH):
                w, rw = wg[oc % 2], P.R("wg", oc % 2)
                for br in range(3):
                    c0 = C_G + br * 1024 + oc * 128
                    self.load_w(w[:, :, br, :], rw, I["w_in"][l][:, c0:c0 + 128])
                wb_, rwb = wpb[oc % 2], P.R("wpb", oc % 2)
                self.load_w(wb_[:], rwb, I["w_pb"][l][:, oc * 128:(oc + 1) * 128])
                for (t0, n) in tiles:
                    pa_t, rpa = pab[0], P.R("pab", 0)
                    pc_t, rpc = pcb[0], P.R("pcb", 0)
                    ac, rac = acc2[0], P.R("acc2", 0)
                    it += 1
                    if use_a:
                        P.dma("sp", pa_t[:, 0:n], self.PA[oc, :, t0:t0 + n], reads=[P.R("PA", oc, t0)], writes=[rpa])
                    if use_c:
                        P.dma("sp", pc_t[:, 0:n], self.PC[oc, :, t0:t0 + n], reads=[P.R("PC", oc, t0)], writes=[rpc])
                    gps = []
                    for br in range(3):
                        if (br == 0 and not use_a) or (br == 2 and not use_c):
                            gps.append(None)
                            continue
                        ps, rp = self.next_ps()
                        for k in range(NCH):
                            P.op("pe", lambda e, k=k, br=br, ps=ps: e.matmul(ps[:, 0:n], w[:, k, br, :], self.hT[:, k, t0:t0 + n],
                                                                            start=(k == 0), stop=(k == NCH - 1)), reads=[rw] + hall, writes=[rp])
                        P.op("act", lambda e, br=br, ps=ps: e.activation(out=sig[br][:, 0:n], in_=ps[:, 0:n], func=AF.Sigmoid),
                             reads=[rp], writes=[sigr[br]])
                    ps, rp = self.next_ps()
                    for k in range(4):
                        P.op("pe", lambda e, k=k, ps=ps: e.matmul(ps[:, 0:n], wb_[:, k, :], ybT[:, k, t0:t0 + n], start=(k == 0), stop=(k == 3)),
                             reads=[rwb] + rybs, writes=[rp])
                    last = not (use_a or use_c)
                    dst = preT[:, oc, t0:t0 + n] if last else ac[:, 0:n]
                    P.op("dve", lambda e, ps=ps, dst=dst: e.tensor_tensor(out=dst, in0=ps[:, 0:n], in1=sig[1][:, 0:n], op=ALU.mult),
                         reads=[rp, sigr[1]], writes=[P.R("preT", oc) if last else rac])
                    if use_a:
                        P.op("pool", lambda e: e.tensor_tensor(out=pa_t[:, 0:n], in0=pa_t[:, 0:n], in1=sig[0][:, 0:n], op=ALU.mult),
                             reads=[rpa, sigr[0]], writes=[rpa])
                        last = not use_c
                        dst = preT[:, oc, t0:t0 + n] if last else ac[:, 0:n]
                        P.op("dve", lambda e, dst=dst: e.tensor_tensor(out=dst, in0=ac[:, 0:n], in1=pa_t[:, 0:n], op=ALU.add),
                             reads=[rac, rpa], writes=[P.R("preT", oc) if last else rac])
                    if use_c:
                        P.op("pool", lambda e: e.tensor_tensor(out=pc_t[:, 0:n], in0=pc_t[:, 0:n], in1=sig[2][:, 0:n], op=ALU.mult),
                             reads=[rpc, sigr[2]], writes=[rpc])
                        P.op("dve", lambda e: e.tensor_tensor(out=preT[:, oc, t0:t0 + n], in0=ac[:, 0:n], in1=pc_t[:, 0:n], op=ALU.add),
                             reads=[rac, rpc], writes=[P.R("preT", oc)])
        with P.scope():
            wo = [P.sbuf("wo%d" % i, [128, NCH, 128], BF16) for i in range(2)]
            rpre = [P.R("preT", oc) for oc in range(NCH)]
            for o2 in range(NCH):
                w, rw = wo[o2 % 2], P.R("wo", o2 % 2)
                self.load_w(w[:], rw, I["w_o"][l][:, o2 * 128:(o2 + 1) * 128])
                for (t0, n) in tiles:
                    col = nb if t0 < LCTX else b
                    ps, rp = self.next_ps()
                    for k in range(NCH):
                        P.op("pe", lambda e, k=k, ps=ps: e.matmul(ps[:, 0:n], w[:, k, :], preT[:, k, t0:t0 + n], start=(k == 0), stop=(k == NCH - 1)),
                             reads=[rw] + rpre, writes=[rp])
                    xr = self.xres(o2, t0, n)
                    P.op("dve", lambda e, ps=ps: e.scalar_tensor_tensor(out=self.xT[:, o2, t0:t0 + n], in0=ps[:, 0:n], scalar=self.mod[:, l, 16 + o2, col:col + 1],
                                                                        in1=self.xT[:, o2, t0:t0 + n], op0=ALU.mult, op1=ALU.add),
                         reads=[rp, P.R("params")] + xr, writes=xr)

    def emit_ffn(self, l, b):
        P = self.P
        I = self.inp
        need_ctx = l < DEPTH - 1
        cols = [self.nb, b]
        tiles = []
        if need_ctx:
            tiles.append((0, LCTX, self.nb, 0, LCTX))
        for t0 in range(LCTX, NT, 512):
            tiles.append((t0, 512, b, LCTX, NT))
        for (t0, n, col, s0, s1) in tiles:
            self.emit_ffn_tile(l, t0, n, col, s0, s1)

    def emit_ffn_tile(self, l, t0, n, col, s0, s1):
        P = self.P
        I = self.inp
        h2 = self.h2
        rh2 = P.R("h2")
        self.emit_norm(t0, n, lambda c: self.s2[:, l, col, c:c + 1], lambda c: self.mod[:, l, 24 + c, col:col + 1], h2, rh2)
        hl = t0 - 1 >= s0
        hr = t0 + n < s1
        if hl:
            P.op("dve", lambda e: e.tensor_copy(out=h2[:, :, 512:513], in_=self.hcarry[:, :, 0:1]), reads=[P.R("hcarry")], writes=[rh2])
        else:
            P.op("dve", lambda e: e.memset(h2[:, :, 512:513], 0.0), writes=[rh2])
        P.op("dve", lambda e: e.tensor_copy(out=self.hcarry[:, :, 0:1], in_=h2[:, :, n - 1:n]), reads=[rh2], writes=[P.R("hcarry")])
        if hr:
            self.emit_norm(t0 + n, 1, lambda c: self.s2[:, l, col, c:c + 1], lambda c: self.mod[:, l, 24 + c, col:col + 1], h2, rh2, dst_off=513)
        else:
            P.op("dve", lambda e: e.memset(h2[:, :, 513:514], 0.0), writes=[rh2])
        act = self.actb
        ract = P.R("actb")
        for blk in range(NFF // 2):
            wb = self.wU[self.wU_i % 2]
            rw = P.R("wU", self.wU_i % 2)
            self.wU_i += 1
            for ab in range(2):
                c0 = ab * DFF + blk * 256
                P.dma("pool", wb[:, :, ab, :], I["w_up"][l][:, c0:c0 + 256].rearrange("(c p) n -> p c n", p=128), writes=[rw])
            for ii in range(2):
                i = blk * 2 + ii
                accs = []
                for ab in range(2):
                    fi = ab * NFF + i
                    ps, rp = self.next_ps()
                    for k in range(NCH):
                        P.op("pe", lambda e, k=k, ab=ab, ii=ii: e.matmul(ps[:, 0:n], wb[:, k, ab, ii * 128:(ii + 1) * 128], h2[:, k, 0:n],
                                                                         start=(k == 0), stop=(k == NCH - 1)),
                             reads=[rw, rh2], writes=[rp])
                    psh, rph = self.next_ps()
                    for k in range(NCH):
                        P.op("pe", lambda e, k=k, ab=ab, ii=ii: e.matmul(psh[:, 0:2], wb[:, k, ab, ii * 128:(ii + 1) * 128], h2[:, k, 512:514],
                                                                         start=(k == 0), stop=(k == NCH - 1)),
                             reads=[rw, rh2], writes=[rph])
                    acc, racc = self.next_acc()
                    w0 = self.fcwt[:, l, fi, 0:1]
                    w1 = self.fcwt[:, l, fi, 1:2]
                    w2 = self.fcwt[:, l, fi, 2:3]
                    bb = self.fcwt[:, l, fi, 3:4]
                    rpar = P.R("params")
                    P.op("act", lambda e, acc=acc, ps=ps, w1=w1, bb=bb: e.activation(out=acc[:, 0:n], in_=ps[:, 0:n], func=AF.Identity, bias=bb, scale=w1),
                         reads=[rp, rpar], writes=[racc])
                    P.op("dve", lambda e, acc=acc, ps=ps, w0=w0: e.scalar_tensor_tensor(out=acc[:, 1:n], in0=ps[:, 0:n - 1], scalar=w0, in1=acc[:, 1:n],
                                                                                     op0=ALU.mult, op1=ALU.add), reads=[rp, rpar, racc], writes=[racc])
                    P.op("dve", lambda e, acc=acc, ps=ps, w2=w2: e.scalar_tensor_tensor(out=acc[:, 0:n - 1], in0=ps[:, 1:n], scalar=w2, in1=acc[:, 0:n - 1],
                                                                                     op0=ALU.mult, op1=ALU.add), reads=[rp, rpar, racc], writes=[racc])
                    P.op("dve", lambda e, acc=acc, psh=psh, w0=w0: e.scalar_tensor_tensor(out=acc[:, 0:1], in0=psh[:, 0:1], scalar=w0, in1=acc[:, 0:1],
                                                                                      op0=ALU.mult, op1=ALU.add), reads=[rph, rpar, racc], writes=[racc])
                    P.op("dve", lambda e, acc=acc, psh=psh, w2=w2: e.scalar_tensor_tensor(out=acc[:, n - 1:n], in0=psh[:, 1:2], scalar=w2, in1=acc[:, n - 1:n],
                                                                                      op0=ALU.mult, op1=ALU.add), reads=[rph, rpar, racc], writes=[racc])
                    accs.append((acc, racc))
                (aa, ra), (ab_, rb_) = accs
                P.op("act", lambda e, aa=aa: e.activation(out=aa[:, 0:n], in_=aa[:, 0:n], func=AF.Silu), reads=[ra], writes=[ra])
                P.op("dve", lambda e, aa=aa, ab_=ab_, i=i: e.tensor_tensor(out=act[:, i, 0:n], in0=aa[:, 0:n], in1=ab_[:, 0:n], op=ALU.mult),
                     reads=[ra, rb_], writes=[ract])
        for oc in range(NCH):
            wd = self.wD[self.wD_i % 2]
            rwd = P.R("wD", self.wD_i % 2)
            self.wD_i += 1
            P.dma("pool", wd[:], I["w_down"][l][:, oc * 128:(oc + 1) * 128].rearrange("(c p) n -> p c n", p=128), writes=[rwd])
            ps, rp = self.next_ps()
            for i in range(NFF):
                P.op("pe", lambda e, i=i, wd=wd: e.matmul(ps[:, 0:n], wd[:, i, :], act[:, i, 0:n], start=(i == 0), stop=(i == NFF - 1)),
                     reads=[rwd, ract], writes=[rp])
            xr = self.xres(oc, t0, n)
            P.op("dve", lambda e, oc=oc, ps=ps: e.scalar_tensor_tensor(out=self.xT[:, oc, t0:t0 + n], in0=ps[:, 0:n], scalar=self.mod[:, l, 40 + oc, col:col + 1],
                                                                       in1=self.xT[:, oc, t0:t0 + n], op0=ALU.mult, op1=ALU.add),
                 reads=[rp, P.R("params")] + xr, writes=xr)

    def next_acc(self):
        i = self.acc_i
        self.acc_i = (i + 1) % len(self.accb)
        return self.accb[i], self.P.R("acc", i)


def _alloc_ffn(self):
    P = self.P
    self.h2 = P.sbuf("h2", [128, NCH, 514], BF16)
    self.hcarry = P.sbuf("hcarry", [128, NCH, 2], BF16)
    self.actb = P.sbuf("actb", [128, NFF, 512], BF16)
    self.wU = [P.sbuf("wU%d" % i, [128, NCH, 2, 256], BF16) for i in range(2)]
    self.wU_i = 0
    self.wD = [P.sbuf("wD%d" % i, [128, NFF, 128], BF16) for i in range(2)]
    self.wD_i = 0
    self.accb = [P.sbuf("acc%d" % i, [128, 512], F32) for i in range(4)]
    self.acc_i = 0


FWD_ORDER = list(range(18))
BWD_ORDER = [1, 0] + list(range(17, 1, -1))


def run_interleaved(gens):
    gens = list(gens)
    while gens:
        nxt = []
        for g in gens:
            try:
                next(g)
                nxt.append(g)
            except StopIteration:
                pass
        gens = nxt


def _emit_gdn_A(self, l, b):
    P = self.P
    I = self.inp
    nb = self.nb
    need_ctx = l < DEPTH - 1
    hall = self.hres
    G, BT, NBT = self.gG, self.gBT, self.gNBT
    if True:
        self.alloc_rows()
        PW = NT + 4
        ubuf = P.sbuf("g_ubuf", [128, PW], F32)
        cbuf = P.sbuf("g_cbuf", [128, PW], F32)
        sqb = P.sbuf("g_sqb", [128, PW], BF16)
        qnb = P.sbuf("g_qnb", [128, PW], BF16)
        obuf = [P.sbuf("g_obuf%d" % i, [128, NT], BF16) for i in range(2)]
        wgd = [P.sbuf("g_w%d" % i, [128, NCH, 128], BF16) for i in range(2)]
        cw = P.sbuf("g_cw", [128, 12, 3], F32)
        rtab = [P.sbuf("g_rt%d" % i, [128, 2, 512], F32) for i in range(2)]
        sperm = P.sbuf("g_sperm", [128, 128], BF16)
        sp32 = P.sbuf("g_sp32", [128, 128], F32)
        wab = P.sbuf("g_wab", [128, NCH, 16], BF16)
        rowv = P.sbuf("g_rowv", [1, 32], F32)
        one1 = P.sbuf("g_one1", [1, 128], F32)
        onec = P.sbuf("g_onec", [128, 1], F32)
        nega = P.sbuf("g_nega", [128, 8], F32)
        e1 = P.sbuf("g_e1", [128, 18, 8], F32)
        r_u, r_c, r_sq, r_qn = P.R("g_ubuf"), P.R("g_cbuf"), P.R("g_sqb"), P.R("g_qnb")
        P.op("dve", lambda e: e.memset(ubuf[:], 0.0), writes=[r_u])
        P.op("dve", lambda e: e.memset(one1[:], 1.0), writes=[P.R("g_one1")])
        P.op("dve", lambda e: e.memset(onec[:], 1.0), writes=[P.R("g_onec")])
        P.dma("sp", sp32[:], I["sperm"], writes=[P.R("g_sp32")])
        P.op("dve", lambda e: e.tensor_copy(out=sperm[:], in_=sp32[:]), reads=[P.R("g_sp32")], writes=[P.R("g_sperm")])

        def mk(psv, c0, ncb, rp):
            P.op("dve", lambda e: e.tensor_copy(out=cw[:, c0:c0 + ncb, :], in_=psv), reads=[rp], writes=[P.R("g_cw")])
        self.load_rows_T(I["dn_conv_w"][l], 3, 12, mk, "dncw")
        P.op("dve", lambda e: e.memset(rowv[:], 0.0), writes=[P.R("g_rowv")])
        P.dma("sp", rowv[0:1, 0:8], I["dn_dt_bias"][l:l + 1, :], writes=[P.R("g_rowv")])
        P.dma("sp", rowv[0:1, 16:24], I["dn_a_log"][l:l + 1, :], writes=[P.R("g_rowv")])
        self.load_w(wab[:], P.R("g_wab"), I["w_in"][l][:, C_A:C_A + 16])
        ps, rp = self.next_ps()
        for j in range(18):
            for k in range(NCH):
                P.op("pe", lambda e, j=j, k=k: e.matmul(ps[:, j * 16:(j + 1) * 16], self.hT[:, k, j * 128:(j + 1) * 128], wab[:, k, :],
                                                        start=(k == 0), stop=False), reads=[P.R("g_wab")] + hall, writes=[rp])
            P.op("pe", lambda e, j=j: e.matmul(ps[:, j * 16:(j + 1) * 16], one1[0:1, :], rowv[0:1, 0:16], start=False, stop=True),
                 reads=[P.R("g_one1"), P.R("g_rowv")], writes=[rp])
        ps2, rp2 = self.next_ps()
        P.op("pe", lambda e: e.matmul(ps2[:, 0:8], one1[0:1, :], rowv[0:1, 16:24], start=True, stop=True),
             reads=[P.R("g_one1"), P.R("g_rowv")], writes=[rp2])
        P.op("act", lambda e: e.activation(out=nega[:], in_=ps2[:, 0:8], func=AF.Exp), reads=[rp2], writes=[P.R("g_nega")])
        psv = ps[:, 0:288].rearrange("p (j f) -> p j f", f=16)
        P.op("act", lambda e: e.activation(out=e1[:], in_=psv[:, :, 0:8], func=AF.Exp), reads=[rp], writes=[P.R("g_e1")])
        P.op("act", lambda e: e.activation(out=e1[:], in_=e1[:], func=AF.Ln, bias=onec[:], scale=1.0), reads=[P.R("g_e1"), P.R("g_onec")], writes=[P.R("g_e1")])
        P.op("dve", lambda e: e.scalar_tensor_tensor(out=G[:], in0=e1[:], scalar=-1.0, in1=nega[:].unsqueeze(1).broadcast_to([128, 18, 8]),
                                                    op0=ALU.mult, op1=ALU.mult), reads=[P.R("g_e1"), P.R("g_nega")], writes=[P.R("gdnG")])
        P.op("act", lambda e: e.activation(out=BT[:], in_=psv[:, :, 8:16], func=AF.Sigmoid), reads=[rp], writes=[P.R("gdnB")])
        P.op("dve", lambda e: e.tensor_scalar(out=NBT[:], in0=BT[:], scalar1=-1.0, scalar2=None, op0=ALU.mult), reads=[P.R("gdnB")], writes=[P.R("gdnB")])
        if l == 0 and b == 0:
            self.dump("gdnG", G[:], [128, 18, 8], [P.R("gdnG")])
            self.dump("gdnB", BT[:], [128, 18, 8], [P.R("gdnB")])
        off = lambda t: t + 1 if t < LCTX else t + 3
        rti = 0
        for fc in range(16):
            kind = fc // 4
            hd = fc % 4
            c0 = (C_Z + hd * 128) if kind == 3 else fc * 128
            w, rw = wgd[fc % 2], P.R("g_w", fc % 2)
            self.load_w(w[:], rw, I["w_in"][l][:, c0:c0 + 128])
            ob, rob = obuf[fc % 2], P.R("g_obuf", fc % 2)
            for (t0, n) in token_tiles():
                ps, rp = self.next_ps()
                for k in range(NCH):
                    P.op("pe", lambda e, k=k, ps=ps: e.matmul(ps[:, 0:n], w[:, k, :], self.hT[:, k, t0:t0 + n], start=(k == 0), stop=(k == NCH - 1)),
                         reads=[rw] + hall, writes=[rp])
                if kind == 3:
                    P.op("act", lambda e, ps=ps: e.activation(out=ob[:, t0:t0 + n], in_=ps[:, 0:n], func=AF.Silu), reads=[rp], writes=[rob])
                else:
                    o = off(t0)
                    P.op("act", lambda e, ps=ps, o=o: e.copy(out=ubuf[:, o:o + n], in_=ps[:, 0:n]), reads=[rp], writes=[r_u])
            if kind == 3:
                dstD = self.ZD
                P.dma("sp", dstD[:, :, hd, :].rearrange("j p t -> p j t"), ob[:].rearrange("p (j t) -> p j t", t=128), reads=[rob], writes=[P.R("ZD")])
                continue
            P.op("act", lambda e, fc=fc: e.activation(out=cbuf[:], in_=ubuf[:], func=AF.Copy, scale=cw[:, fc, 1:2]), reads=[r_u, P.R("g_cw")], writes=[r_c])
            P.op("dve", lambda e, fc=fc: e.scalar_tensor_tensor(out=cbuf[:, 1:PW], in0=ubuf[:, 0:PW - 1], scalar=cw[:, fc, 0:1], in1=cbuf[:, 1:PW],
                                                           op0=ALU.mult, op1=ALU.add), reads=[r_u, r_c, P.R("g_cw")], writes=[r_c])
            P.op("dve", lambda e, fc=fc: e.scalar_tensor_tensor(out=cbuf[:, 0:PW - 1], in0=ubuf[:, 1:PW], scalar=cw[:, fc, 2:3], in1=cbuf[:, 0:PW - 1],
                                                           op0=ALU.mult, op1=ALU.add), reads=[r_u, r_c, P.R("g_cw")], writes=[r_c])
            P.op("act", lambda e: e.activation(out=cbuf[:], in_=cbuf[:], func=AF.Silu), reads=[r_c], writes=[r_c])
            if kind == 2:
                P.op("dve", lambda e: e.tensor_copy(out=ob[:, 0:LCTX], in_=cbuf[:, 1:1 + LCTX]), reads=[r_c], writes=[rob])
                P.op("act", lambda e: e.copy(out=ob[:, LCTX:NT], in_=cbuf[:, 3 + LCTX:3 + NT]), reads=[r_c], writes=[rob])
                P.dma("sp", self.VD[:, :, hd, :].rearrange("j p t -> p j t"), ob[:].rearrange("p (j t) -> p j t", t=128), reads=[rob], writes=[P.R("VD")])
                continue
            P.op("act", lambda e: e.activation(out=sqb[:], in_=cbuf[:], func=AF.Square), reads=[r_c], writes=[r_sq])
            qscale = 128.0 ** -0.5 if kind == 0 else 1.0
            for (t0, n) in token_tiles():
                o = off(t0)
                ps, rp = self.next_ps()
                P.op("pe", lambda e, ps=ps, o=o: e.matmul(ps[:, 0:n], self.onesb[:], sqb[:, o:o + n], start=True, stop=True),
                     reads=[r_sq, P.R("onesb")], writes=[rp])
                P.op("act", lambda e, ps=ps: e.activation(out=self.rstd[:, 0:n], in_=ps[:, 0:n], func=AF.Sqrt, bias=self.eps[:], scale=1.0),
                     reads=[rp, P.R("eps")], writes=[P.R("rstd")])
                P.op("dve", lambda e: e.reciprocal(out=self.rstd[:, 0:n], in_=self.rstd[:, 0:n]), reads=[P.R("rstd")], writes=[P.R("rstd")])
                if t0 < LCTX:
                    P.op("dve", lambda e, o=o: e.scalar_tensor_tensor(out=ob[:, t0:t0 + n], in0=cbuf[:, o:o + n], scalar=qscale, in1=self.rstd[:, 0:n],
                                                                 op0=ALU.mult, op1=ALU.mult), reads=[r_c, P.R("rstd")], writes=[rob])
                    continue
                P.op("dve", lambda e, o=o: e.scalar_tensor_tensor(out=cbuf[:, o:o + n], in0=cbuf[:, o:o + n], scalar=qscale, in1=self.rstd[:, 0:n],
                                                             op0=ALU.mult, op1=ALU.mult), reads=[r_c, P.R("rstd")], writes=[r_c])
                P.op("act", lambda e, o=o: e.copy(out=qnb[:, o:o + n], in_=cbuf[:, o:o + n]), reads=[r_c], writes=[r_qn])
                rt, rrt = rtab[rti % 2], P.R("g_rt", rti % 2)
                rti += 1
                tl0 = t0 - LCTX
                P.dma("sp", rt[:, 0, 0:n], I["rope_cos"][:, tl0:tl0 + n], writes=[rrt])
                P.dma("sp", rt[:, 1, 0:n], I["rope_sin"][:, tl0:tl0 + n], writes=[rrt])
                ps2, rp2 = self.next_ps()
                P.op("pe", lambda e, ps2=ps2, o=o: e.matmul(ps2[:, 0:n], sperm[:], qnb[:, o:o + n], start=True, stop=True),
                     reads=[r_qn, P.R("g_sperm")], writes=[rp2])
                tmp, rtm = self.next_tmp()
                P.op("dve", lambda e, ps2=ps2, tmp=tmp, rt=rt: e.tensor_tensor(out=tmp[:, 0:n], in0=ps2[:, 0:n], in1=rt[:, 1, 0:n], op=ALU.mult),
                     reads=[rp2, rrt], writes=[rtm])
                P.op("pool", lambda e, o=o, rt=rt: e.tensor_tensor(out=cbuf[:, o:o + n], in0=cbuf[:, o:o + n], in1=rt[:, 0, 0:n], op=ALU.mult),
                     reads=[r_c, rrt], writes=[r_c])
                P.op("dve", lambda e, o=o, tmp=tmp: e.tensor_tensor(out=ob[:, t0:t0 + n], in0=cbuf[:, o:o + n], in1=tmp[:, 0:n], op=ALU.add),
                     reads=[r_c, rtm], writes=[rob])
            dstD = self.QD if kind == 0 else self.KD
            P.dma("sp", dstD[:, :, hd, :].rearrange("j p t -> p j t"), ob[:].rearrange("p (j t) -> p j t", t=128), reads=[rob],
                  writes=[P.R("QD" if kind == 0 else "KD")])
    if l == 0 and b == 0:
        for nm in ("QD", "KD", "VD", "ZD"):
            self.dump(nm, getattr(self, nm), [18, 128, 4, 128], [P.R(nm)], BF16)


def _emit_gdn_B(self, l, b):
    P = self.P
    I = self.inp
    nb = self.nb
    need_ctx = l < DEPTH - 1
    G, BT, NBT = self.gG, self.gBT, self.gNBT
    yaT = P.sbuf("yaT", [128, 4, NT], BF16)
    ryat = [P.R("yaT", j) for j in range(18)]
    ones32 = P.sbuf("ones32", [128, 128], F32)
    P.op("dve", lambda e: e.memset(ones32[:], 1.0), writes=[P.R("ones32")])
    with P.scope():
        msk = P.sbuf("g_msk", [128, 8, 128], F32)
        P.dma("sp", msk[:], I["gdn_masks"].rearrange("m p c -> p m c"), writes=[P.R("g_msk")])
        ng = P.sbuf("g_ng", [128, 1], F32)
        P.dma("sp", ng[:], I["dn_norm_g"][l].rearrange("(p o) -> p o", o=1), writes=[P.R("g_ng")])
        S32 = P.sbuf("g_S32", [128, 8, 128], F32)
        Sb = P.sbuf("g_Sb", [128, 8, 128], BF16)
        P.op("dve", lambda e: e.memset(S32[:], 0.0), writes=[P.R("g_S32", d) for d in range(2)])
        P.op("dve", lambda e: e.memset(Sb[:], 0.0), writes=[P.R("g_Sb", d) for d in range(2)])
        osum_ = [P.sbuf("g_osum%d" % d, [128, 4, 128], F32) for d in range(2)]
        zt_ = [P.sbuf("g_zt%d" % d, [128, 4, 128], BF16) for d in range(2)]
        fsq_ = [P.sbuf("g_fsq%d" % d, [128, 512], BF16) for d in range(2)]
        frs_ = [P.sbuf("g_frs%d" % d, [128, 512], F32) for d in range(2)]

        def B1(nm):
            return P.sbuf(nm, [128, 4, 128], BF16)
        bufs = {}
        for d in range(2):
            for sl in range(2):
                k = (d, sl)
                bufs[k] = dict(qt=B1("g_qt%d%d" % k), kt=B1("g_kt%d%d" % k), vt=B1("g_vt%d%d" % k), ktok=B1("g_ktok%d%d" % k),
                               vb=B1("g_vb%d%d" % k), kd=B1("g_kd%d%d" % k), qd=B1("g_qd%d%d" % k), TT=B1("g_TT%d%d" % k),
                               qkd=B1("g_qkd%d%d" % k), erow=P.sbuf("g_erow%d%d" % k, [128, 4, 128], F32),
                               nbeg=P.sbuf("g_nbeg%d%d" % k, [128, 4], F32))
            bufs[d] = dict(gcs=P.sbuf("g_gcs%d" % d, [128, 8], F32), ekd=P.sbuf("g_ekd%d" % d, [128, 4], F32),
                           GM=P.sbuf("g_GM%d" % d, [128, 4, 128], F32), arg=P.sbuf("g_arg%d" % d, [128, 4, 128], F32),
                           dec=P.sbuf("g_dec%d" % d, [128, 4, 128], BF16), decT=P.sbuf("g_decT%d" % d, [128, 4, 128], BF16),
                           Nk=[B1("g_N%d%d" % (d, i)) for i in range(2)], Mk=[B1("g_M%d%d" % (d, i)) for i in range(2)],
                           Pk=[B1("g_P%d%d" % (d, i)) for i in range(2)], R=P.sbuf("g_R%d" % d, [128, 4, 128], BF16),
                           vn=P.sbuf("g_vn%d" % d, [128, 4, 128], BF16))
        orders = [FWD_ORDER, BWD_ORDER]
        fstep = {t: i for i, t in enumerate(FWD_ORDER)}
        bstep = {t: i for i, t in enumerate(BWD_ORDER)}
        DQ = [self.QD, self.KD, self.VD]

        def prep(d, n, banks):
            tile = orders[d][n]
            sl = n % 2
            bf = bufs[(d, sl)]
            bd = bufs[d]
            key = ("gprep", d, sl)
            rin, rout = P.R("g_in", d, sl), P.R("g_out", d, sl)
            rloc = P.R("g_loc", d)
            MI, MA, MN, MT = (msk[:, 4 * d + i, :] for i in range(4))
            rm = P.R("g_msk")
            gcol = lambda h: G[:, tile, 4 * d + h:4 * d + h + 1]
            pA, rpA = self.ps[banks[0]], P.R("ps", banks[0])
            pB, rpB = self.ps[banks[1]], P.R("ps", banks[1])
            v4 = lambda ps: ps[:, :].rearrange("p (h c) -> p h c", c=128)
            P.dma("sp", bf["qt"][:], self.QD[tile], reads=[P.R("QD")], writes=[rin])
            P.dma("sp", bf["kt"][:], self.KD[tile], reads=[P.R("KD")], writes=[rin])
            P.dma("sp", bf["vt"][:], self.VD[tile], reads=[P.R("VD")], writes=[rin])
            yield
            for h in range(4):
                P.op("pe", lambda e, h=h: e.matmul(pA[:, h * 128:(h + 1) * 128], bf["kt"][:, h, :], self.identb[:], start=True, stop=True),
                     reads=[rin, P.R("identb")], writes=[rpA])
            for h in range(4):
                P.op("pe", lambda e, h=h: e.matmul(pB[:, h * 128:(h + 1) * 128], bf["vt"][:, h, :], self.identb[:], start=True, stop=True),
                     reads=[rin, P.R("identb")], writes=[rpB])
            yield
            P.op("act", lambda e: e.copy(out=bf["ktok"][:], in_=v4(pA)), reads=[rpA], writes=[rout])
            for h in range(4):
                P.op("dve", lambda e, h=h: e.tensor_scalar(out=bf["vb"][:, h, :], in0=pB[:, h * 128:(h + 1) * 128], scalar1=BT[:, tile, 4 * d + h:4 * d + h + 1],
                                                           scalar2=None, op0=ALU.mult), reads=[rpB, P.R("gdnB")], writes=[rout])
            P.op("pe", lambda e: e.matmul(pA[:, 0:4], MI, G[:, tile, 4 * d:4 * d + 4], start=True, stop=True), reads=[rm, P.R("gdnG")], writes=[rpA])
            P.op("pe", lambda e: e.matmul(pA[:, 4:8], MA, G[:, tile, 4 * d:4 * d + 4], start=True, stop=True), reads=[rm, P.R("gdnG")], writes=[rpA])
            for h in range(4):
                P.op("pool", lambda e, h=h: e.tensor_scalar(out=bd["GM"][:, h, :], in0=MI, scalar1=gcol(h), scalar2=None, op0=ALU.mult),
                     reads=[rm, P.R("gdnG")], writes=[rloc])
            yield
            P.op("dve", lambda e: e.tensor_copy(out=bd["gcs"][:], in_=pA[:, 0:8]), reads=[rpA], writes=[rloc])
            for h in range(4):
                P.op("pe", lambda e, h=h: e.matmul(pB[:, h * 128:(h + 1) * 128], ones32[:], bd["GM"][:, h, :], start=True, stop=True),
                     reads=[rloc, P.R("ones32")], writes=[rpB])
            yield
            P.op("act", lambda e: e.activation(out=bf["nbeg"][:], in_=bd["gcs"][:, 0:4], func=AF.Exp), reads=[rloc], writes=[rout])
            P.op("dve", lambda e: e.tensor_tensor(out=bf["nbeg"][:], in0=bf["nbeg"][:], in1=NBT[:, tile, 4 * d:4 * d + 4], op=ALU.mult),
                 reads=[rout, P.R("gdnB")], writes=[rout])
            P.op("act", lambda e: e.activation(out=bd["ekd"][:], in_=bd["gcs"][:, 4:8], func=AF.Exp), reads=[rloc], writes=[rloc])
            P.op("act", lambda e: e.activation(out=bf["erow"][:], in_=v4(pB), func=AF.Exp), reads=[rpB], writes=[rout])
            for h in range(4):
                P.op("pool", lambda e, h=h: e.tensor_scalar(out=bf["kd"][:, h, :], in0=bf["ktok"][:, h, :], scalar1=bd["ekd"][:, h:h + 1], scalar2=None, op0=ALU.mult),
                     reads=[rout, rloc], writes=[rout])
            for h in range(4):
                P.op("dve", lambda e, h=h: e.scalar_tensor_tensor(out=bd["arg"][:, h, :], in0=pB[:, h * 128:(h + 1) * 128], scalar=bd["gcs"][:, h:h + 1], in1=MN,
                                                              op0=ALU.subtract, op1=ALU.max), reads=[rpB, rloc, rm], writes=[rloc])
            yield
            P.op("act", lambda e: e.activation(out=bd["dec"][:], in_=bd["arg"][:], func=AF.Exp, scale=-1.0), reads=[rloc], writes=[rloc])
            P.op("dve", lambda e: e.tensor_tensor(out=bf["qd"][:], in0=bf["qt"][:], in1=bf["erow"][:], op=ALU.mult), reads=[rin, rout], writes=[rout])
            for h in range(4):
                P.op("dve", lambda e, h=h: e.scalar_tensor_tensor(out=bd["arg"][:, h, :], in0=pB[:, h * 128:(h + 1) * 128], scalar=bd["gcs"][:, h:h + 1], in1=MT,
                                                              op0=ALU.subtract, op1=ALU.min), reads=[rpB, rloc, rm], writes=[rloc])
            for h in range(4):
                P.op("pe", lambda e, h=h: e.matmul(pA[:, h * 128:(h + 1) * 128], bf["kt"][:, h, :], bf["kt"][:, h, :], start=True, stop=True),
                     reads=[rin], writes=[rpA])
            yield
            P.op("act", lambda e: e.activation(out=bd["decT"][:], in_=bd["arg"][:], func=AF.Exp), reads=[rloc], writes=[rloc])
            N0 = bd["Nk"][0]
            for h in range(4):
                P.op("dve", lambda e, h=h: e.scalar_tensor_tensor(out=N0[:, h, :], in0=pA[:, h * 128:(h + 1) * 128], scalar=NBT[:, tile, 4 * d + h:4 * d + h + 1],
                                                              in1=bd["dec"][:, h, :], op0=ALU.mult, op1=ALU.mult), reads=[rpA, rloc, P.R("gdnB")], writes=[rloc])
            for h in range(4):
                P.op("pe", lambda e, h=h: e.matmul(pB[:, h * 128:(h + 1) * 128], bf["kt"][:, h, :], bf["qt"][:, h, :], start=True, stop=True),
                     reads=[rin], writes=[rpB])
            yield
            P.op("dve", lambda e: e.tensor_tensor(out=bf["qkd"][:], in0=v4(pB), in1=bd["decT"][:], op=ALU.mult), reads=[rpB, rloc], writes=[rout])
            for h in range(4):
                P.op("pe", lambda e, h=h: e.matmul(pA[:, h * 128:(h + 1) * 128], N0[:, h, :], self.identb[:], start=True, stop=True),
                     reads=[rloc, P.R("identb")], writes=[rpA])
            for h in range(4):
                P.op("pe", lambda e, h=h: e.matmul(pB[:, h * 128:(h + 1) * 128], N0[:, h, :], self.identb[:], start=True, stop=False),
                     reads=[rloc, P.R("identb")], writes=[rpB])
                P.op("pe", lambda e, h=h: e.matmul(pB[:, h * 128:(h + 1) * 128], self.identb[:], self.identb[:], start=False, stop=True),
                     reads=[P.R("identb")], writes=[rpB])
            yield
            Mc = bd["Mk"][0]
            Pc = bd["Pk"][0]
            Nc = N0
            P.op("act", lambda e: e.copy(out=Mc[:], in_=v4(pA)), reads=[rpA], writes=[rloc])
            P.op("dve", lambda e: e.tensor_copy(out=Pc[:], in_=v4(pB)), reads=[rpB], writes=[rloc])
            for lev in range(5):
                Nn = bd["Nk"][(lev + 1) % 2]
                Mn = bd["Mk"][(lev + 1) % 2]
                Pn = bd["Pk"][(lev + 1) % 2] if lev < 4 else bf["TT"]
                for h in range(4):
                    P.op("pe", lambda e, h=h, Mc=Mc, Nc=Nc: e.matmul(pA[:, h * 128:(h + 1) * 128], Mc[:, h, :], Nc[:, h, :], start=True, stop=True),
                         reads=[rloc], writes=[rpA])
                if lev < 4:
                    for h in range(4):
                        P.op("pe", lambda e, h=h, Mc=Mc, Nc=Nc: e.matmul(pB[:, h * 128:(h + 1) * 128], Nc[:, h, :], Mc[:, h, :], start=True, stop=True),
                             reads=[rloc], writes=[rpB])
                yield
                P.op("act", lambda e, Nn=Nn: e.copy(out=Nn[:], in_=v4(pA)), reads=[rpA], writes=[rloc])
                if lev < 4:
                    P.op("dve", lambda e, Mn=Mn: e.tensor_copy(out=Mn[:], in_=v4(pB)), reads=[rpB], writes=[rloc])
                for h in range(4):
                    P.op("pe", lambda e, h=h, Pc=Pc: e.matmul(pA[:, h * 128:(h + 1) * 128], self.identb[:], Pc[:, h, :], start=True, stop=False),
                         reads=[rloc, P.R("identb")], writes=[rpA])
                    P.op("pe", lambda e, h=h, Pc=Pc, Nn=Nn: e.matmul(pA[:, h * 128:(h + 1) * 128], Nn[:, h, :], Pc[:, h, :], start=False, stop=True),
                         reads=[rloc], writes=[rpA])
                yield
                P.op("dve", lambda e, Pn=Pn: e.tensor_copy(out=Pn[:], in_=v4(pA)), reads=[rpA], writes=[rout if lev == 4 else rloc])
                Nc, Mc, Pc = Nn, Mn, Pn
            yield

        def scan(d, n, banks):
            tile = orders[d][n]
            sl = n % 2
            bf = bufs[(d, sl)]
            bd = bufs[d]
            rin, rout = P.R("g_in", d, sl), P.R("g_out", d, sl)
            rS32, rSb = P.R("g_S32", d), P.R("g_Sb", d)
            rR, rvn = P.R("g_R", d), P.R("g_vn", d)
            pA, rpA = self.ps[banks[0]], P.R("ps", banks[0])
            pB, rpB = self.ps[banks[1]], P.R("ps", banks[1])
            other = bstep[tile] if d == 0 else fstep[tile]
            first = n < other
            osum, zt, fsq, frs = osum_[d], zt_[d], fsq_[d], frs_[d]
            want_o = need_ctx or tile >= 2
            for ci in ([0, 1] if d == 0 else [1, 0]):
                rows = slice(ci * 64, ci * 64 + 64)
                glcol = (ci * 64 + 63) if d == 0 else (ci * 64)
                tok = slice(tile * 128 + ci * 64, tile * 128 + ci * 64 + 64)
                for h in range(4):
                    P.op("pe", lambda e, h=h: e.matmul(pA[rows, h * 128:(h + 1) * 128], bf["kt"][:, h, rows], Sb[:, 4 * d + h, :], start=True, stop=True),
                         reads=[rin, rSb], writes=[rpA])
                yield
                for h in range(4):
                    P.op("dve", lambda e, h=h: e.scalar_tensor_tensor(out=bd["R"][rows, h, :], in0=pA[rows, h * 128:(h + 1) * 128], scalar=bf["nbeg"][rows, h:h + 1],
                                                                  in1=bf["vb"][rows, h, :], op0=ALU.mult, op1=ALU.add), reads=[rpA, rout], writes=[rR])
                for h in range(4):
                    P.op("pe", lambda e, h=h: e.matmul(pB[rows, h * 128:(h + 1) * 128], bf["TT"][rows, h, rows], bd["R"][rows, h, :], start=True, stop=True),
                         reads=[rout, rR], writes=[rpB])
                yield
                P.op("act", lambda e: e.copy(out=bd["vn"][rows], in_=pB[rows, :].rearrange("p (h c) -> p h c", c=128)), reads=[rpB], writes=[rvn])
                if want_o:
                    for h in range(4):
                        P.op("pe", lambda e, h=h: e.matmul(pA[:, h * 64:(h + 1) * 64], Sb[:, 4 * d + h, :], bf["qd"][:, h, rows], start=True, stop=False),
                             reads=[rSb, rout], writes=[rpA])
                        P.op("pe", lambda e, h=h: e.matmul(pA[:, h * 64:(h + 1) * 64], bd["vn"][rows, h, :], bf["qkd"][rows, h, rows], start=False, stop=True),
                             reads=[rvn, rout], writes=[rpA])
                for h in range(4):
                    P.op("pe", lambda e, h=h: e.matmul(pB[:, h * 128:(h + 1) * 128], bf["kd"][rows, h, :], bd["vn"][rows, h, :], start=True, stop=True),
                         reads=[rout, rvn], writes=[rpB])
                yield
                if want_o:
                    pov = pA[:, 0:256].rearrange("p (h c) -> p h c", c=64)
                    if first:
                        P.op("act", lambda e: e.copy(out=yaT[:, :, tok], in_=pov), reads=[rpA], writes=[ryat[tile]])
                    else:
                        P.op("dve", lambda e: e.tensor_tensor(out=osum[:, :, rows], in0=pov, in1=yaT[:, :, tok], op=ALU.add),
                             reads=[rpA, ryat[tile]], writes=[P.R("g_osum", d)])
                for h in range(4):
                    P.op("dve", lambda e, h=h: e.scalar_tensor_tensor(out=S32[:, 4 * d + h, :], in0=S32[:, 4 * d + h, :], scalar=bf["erow"][:, h, glcol:glcol + 1],
                                                                  in1=pB[:, h * 128:(h + 1) * 128], op0=ALU.mult, op1=ALU.add), reads=[rS32, rout, rpB], writes=[rS32])
                P.op("act", lambda e: e.copy(out=Sb[:, 4 * d:4 * d + 4, :], in_=S32[:, 4 * d:4 * d + 4, :]), reads=[rS32], writes=[rSb])
                yield
            if want_o and not first:
                tk = slice(tile * 128, tile * 128 + 128)
                rz = P.R("g_zt", d)
                P.dma("sp", zt[:], self.ZD[tile], reads=[P.R("ZD")], writes=[rz])
                ro = P.R("g_osum", d)
                o2 = osum[:, :, :].rearrange("p h c -> p (h c)")
                P.op("act", lambda e: e.activation(out=fsq[:], in_=o2, func=AF.Square), reads=[ro], writes=[P.R("g_fsq", d)])
                P.op("pe", lambda e: e.matmul(pA[:, :], self.onesb[:], fsq[:], start=True, stop=True), reads=[P.R("g_fsq", d), P.R("onesb")], writes=[rpA])
                yield
                P.op("act", lambda e: e.activation(out=frs[:], in_=pA[:, :], func=AF.Sqrt, bias=self.eps[:], scale=1.0 / 128), reads=[rpA, P.R("eps")], writes=[P.R("g_frs", d)])
                P.op("dve", lambda e: e.reciprocal(out=frs[:], in_=frs[:]), reads=[P.R("g_frs", d)], writes=[P.R("g_frs", d)])
                P.op("dve", lambda e: e.scalar_tensor_tensor(out=o2, in0=o2, scalar=ng[:, 0:1], in1=frs[:], op0=ALU.mult, op1=ALU.mult),
                     reads=[ro, P.R("g_frs", d), P.R("g_ng")], writes=[ro])
                P.op("dve", lambda e: e.tensor_tensor(out=yaT[:, :, tk], in0=osum[:], in1=zt[:], op=ALU.mult), reads=[ro, rz], writes=[ryat[tile]])
                yield

        run_interleaved([prep(0, 0, [0, 1]), prep(1, 0, [2, 3])])
        if l == 0 and b == 0:
            for d in range(2):
                for nm in ("TT", "qkd", "kd", "vb", "qd", "ktok"):
                    self.dump("p%d_%s" % (d, nm), bufs[(d, 0)][nm][:], [128, 4, 128], [P.R("g_out", d, 0)], BF16)
                self.dump("p%d_erow" % d, bufs[(d, 0)]["erow"][:], [128, 4, 128], [P.R("g_out", d, 0)])
                self.dump("p%d_nbeg" % d, bufs[(d, 0)]["nbeg"][:], [128, 4], [P.R("g_out", d, 0)])
                self.dump("p%d_gcs" % d, bufs[d]["gcs"][:], [128, 8], [P.R("g_loc", d)])
        for n in range(18):
            gens = [scan(0, n, [4, 5]), scan(1, n, [6, 7])]
            if n + 1 < 18:
                gens += [prep(0, n + 1, [0, 1]), prep(1, n + 1, [2, 3])]
            run_interleaved(gens)
    if l == 0 and b == 0:
        self.dump("yaT", yaT[:], [128, 4, NT], ryat, BF16)
    tiles = [t for t in token_tiles() if need_ctx or t[0] >= LCTX]
    self.proj_to_dram(yaT, ryat, I["w_pa"][l], self.PA, "PA", tiles)


Builder.emit_gdn_A = _emit_gdn_A
Builder.emit_gdn_B = _emit_gdn_B


def _gdn_masks():
    t = np.arange(128)
    same = (t[:, None] // 64) == (t[None, :] // 64)
    le = t[:, None] <= t[None, :]
    lt = t[:, None] < t[None, :]
    ge = t[:, None] >= t[None, :]
    gt = t[:, None] > t[None, :]
    f = np.float32
    m = np.zeros((8, 128, 128), np.float32)
    m[0] = (same & le).astype(f)
    m[1] = (same & gt).astype(f)
    m[2] = np.where(same & gt, 0.0, 1e4)
    m[3] = np.where(same & le, 0.0, -1e4)
    m[4] = (same & ge).astype(f)
    m[5] = (same & lt).astype(f)
    m[6] = np.where(same & lt, 0.0, 1e4)
    m[7] = np.where(same & ge, 0.0, -1e4)
    return m


def _rope_tables():
    t = np.arange(SEQ)
    rows = (t // GRID_W).astype(np.float32)
    cols = (t % GRID_W).astype(np.float32)
    nf = 32
    inv = (np.float32(10000.0) ** (-np.arange(nf, dtype=np.float32) / np.float32(nf))).astype(np.float32)
    ang = np.concatenate([rows[:, None] * inv, cols[:, None] * inv], axis=-1).astype(np.float32)
    d = np.arange(128)
    cosF = np.cos(ang)[:, d // 2].T.astype(np.float32)
    sgn = np.where(d % 2 == 0, -1.0, 1.0).astype(np.float32)
    sinF = (np.sin(ang)[:, d // 2].T * sgn[:, None]).astype(np.float32)
    sperm = np.zeros((128, 128), np.float32)
    sperm[d, d ^ 1] = 1.0
    return np.ascontiguousarray(cosF), np.ascontiguousarray(sinF), sperm


def _na_idx():
    p = np.arange(128)
    kc = p % 64
    half = p // 64
    qc = np.arange(64)
    cidx = np.clip(kc[:, None] - qc[None, :] + 15, 0, 30)
    cstart = np.clip(qc - 8, 0, 48)
    ok = (kc[:, None] >= cstart[None, :]) & (kc[:, None] < cstart[None, :] + 16)
    ridx = np.arange(14)[None, :] + half[:, None]
    return cidx, ok, ridx


def _na_rpb_gather(rpb):
    if "rpb" in _HOSTC and _HOSTC["rpb"][0] is rpb:
        return _HOSTC["rpb"][1]
    cidx, ok, ridx = _na_idx()
    out = rpb[:, :, ridx[:, :, None], cidx[:, None, :]]
    out = np.ascontiguousarray(out.transpose(0, 2, 1, 3, 4), dtype=np.float32)
    _HOSTC["rpb"] = (rpb, out)
    return out


def _na_mask():
    cidx, ok, ridx = _na_idx()
    m = np.where(ok, 0.0, -30000.0).astype(np.float32)
    return np.ascontiguousarray(np.broadcast_to(m[:, None, :], (128, 14, 64)))


_HOSTC = {}


def make_inputs(inputs, core, nb=NB_CORE):
    b0 = core * nb
    f = lambda a: np.ascontiguousarray(a, dtype=np.float32)
    m = {
        "x": f(inputs["x"][b0:b0 + nb]),
        "ctx": f(inputs["ctx"][b0:b0 + nb]),
        "cvec": f(np.concatenate([inputs["c"][b0:b0 + nb], inputs["c_ctx"][None, :]], axis=0)),
        "dn_a_log": f(inputs["dn_a_log"].reshape(DEPTH, 8)),
        "dn_dt_bias": f(inputs["dn_dt_bias"].reshape(DEPTH, 8)),
        "norm1_g": f(np.concatenate([inputs["norm1_g"], inputs["final_norm_g"][None, :]], axis=0)),
        "ffn_conv_w": f(np.concatenate([inputs["ffn_conv_w"], inputs["ffn_conv_b"][:, None, :]], axis=1)),
        "ident": np.eye(128, dtype=np.float32),
        "na_rpb": _na_rpb_gather(inputs["na_rpb"]),
        "na_mask": _na_mask(),
        "gdn_masks": _gdn_masks(),
        "rope_cos": _rope_tables()[0],
        "rope_sin": _rope_tables()[1],
        "sperm": _rope_tables()[2],
    }
    for k in ["norm2_g", "w_ada", "b_ada", "w_in", "dn_conv_w", "dn_norm_g", "sc_conv_w", "w_pa", "w_pb", "w_pc",
              "w_o", "w_up", "w_down"]:
        m[k] = f(inputs[k])
    return m


_CACHE = {}


def kernel(**inputs):
    if "b" not in _CACHE:
        _CACHE["b"] = Builder()
    bld = _CACHE["b"]
    n = 8
    in_maps = []
    for core in range(n):
        m = make_inputs(inputs, core)
        in_maps.append({k: v for k, v in m.items() if k in bld.inp})
    res = run_bass_kernel_spmd(bld.nc, in_maps, core_ids=list(range(n)))
    return np.concatenate([r["out"] for r in res.results], axis=0)
```
